# Optimizing a Trainium2 kernel written in Bass

```python
import math
import jax, jax.numpy as jnp
from jax import lax
import numpy as np

D_MODEL = 1024
BATCH = 2
SEQ = 8192
DEPTH = 4
DEC_BATCH = 128
DEC_SEQ = 1
PAST_LEN = 2048
PAGE_SIZE = 128

N_A_LAYERS = DEPTH // 2
N_B_LAYERS = DEPTH - N_A_LAYERS
CONV_W = 3
N_HEADS = 8
HEAD_DIM = D_MODEL // N_HEADS // 2
QK_DIM = N_HEADS * 2 * HEAD_DIM
V_DIM = 2 * HEAD_DIM
VO_DIM = N_HEADS * V_DIM
ROT_DIM = HEAD_DIM // 4
ROPE_THETA = 500000.0
D_FF = -(-8 * D_MODEL // (3 * 256)) * 256
DEEPNORM_ALPHA = (2 * DEPTH) ** 0.25
DEEPNORM_BETA = (8 * DEPTH) ** -0.25
LN_EPS = 1e-5
Q_BLOCK = 128

kernel_name = "yoco_shortconv_diffattn_step"


def layer_norm(x, g, b):
    xf = x.astype(jnp.float32)
    mu = xf.mean(-1, keepdims=True)
    var = jnp.square(xf - mu).mean(-1, keepdims=True)
    return ((xf - mu) * lax.rsqrt(var + LN_EPS) * g.astype(jnp.float32) + b.astype(jnp.float32)).astype(x.dtype)


def rms_norm(x, g):
    xf = x.astype(jnp.float32)
    return (xf * lax.rsqrt(jnp.mean(xf * xf, -1, keepdims=True) + LN_EPS) * g.astype(jnp.float32)).astype(x.dtype)


def partial_rope(x, pos):
    half = ROT_DIM // 2
    inv = jnp.power(ROPE_THETA, -jnp.arange(0, ROT_DIM, 2, dtype=jnp.float32) / ROT_DIM)
    ang = pos.astype(jnp.float32)[:, None] * inv[None, :]
    cos = jnp.cos(ang)[None, :, None, None, :].astype(x.dtype)
    sin = jnp.sin(ang)[None, :, None, None, :].astype(x.dtype)
    x1 = x[..., :half]
    x2 = x[..., half:ROT_DIM]
    return jnp.concatenate([x1 * cos - x2 * sin, x2 * cos + x1 * sin, x[..., ROT_DIM:]], axis=-1)


def short_conv_mixer(x, conv_state, w_in, conv_w, w_out):
    T = x.shape[1]
    b, c, h = jnp.split(x @ w_in, 3, axis=-1)
    u = c * h
    buf = jnp.concatenate([conv_state.astype(u.dtype), u], axis=1)
    conv = sum(conv_w[j] * buf[:, j:j + T] for j in range(CONV_W))
    return (b * conv) @ w_out, buf[:, T:]


def swiglu(x, w_gate, w_up, w_down):
    return (jax.nn.silu(x @ w_gate) * (x @ w_up)) @ w_down


def diff_softmax_attention(q, k, v, q_pos, k_pos, lam):
    s = jnp.einsum('nqhcd,nkhcd->nhcqk', q, k).astype(jnp.float32) * (HEAD_DIM ** -0.5)
    mask = k_pos[None, :] <= q_pos[:, None]
    p = jax.nn.softmax(jnp.where(mask, s, -jnp.inf), axis=-1)
    a = (p[:, :, 0] - lam * p[:, :, 1]).astype(v.dtype)
    return jnp.einsum('nhqk,nkhe->nqhe', a, v)


def blocked_causal_diff_attention(q, k, v, lam):
    N, T = q.shape[:2]
    nblk = T // Q_BLOCK
    qb = q.reshape(N, nblk, Q_BLOCK, N_HEADS, 2, HEAD_DIM).swapaxes(0, 1)
    k_pos = jnp.arange(T)

    def one_block(args):
        q_blk, start = args
        return diff_softmax_attention(q_blk, k, v, start + jnp.arange(Q_BLOCK), k_pos, lam)

    out = lax.map(one_block, (qb, jnp.arange(nblk) * Q_BLOCK))
    return out.swapaxes(0, 1).reshape(N, T, N_HEADS, V_DIM)


def trunk(x, pos, conv_state, past_k, past_v,
          w_in, conv_w, w_mix_out, w_kv, w_q, w_o,
          lambda_q1, lambda_k1, lambda_q2, lambda_k2, subln_g,
          ln1_g, ln1_b, w_gate, w_up, w_down, ln2_g, ln2_b):
    N, T, _ = x.shape
    new_conv = []
    k_new = v_new = k_all = v_all = k_pos = None
    for l in range(DEPTH):
        if l < N_A_LAYERS:
            y, st = short_conv_mixer(x, conv_state[l], w_in[l], conv_w[l], w_mix_out[l])
            new_conv.append(st)
        else:
            if l == N_A_LAYERS:
                kv = x @ w_kv
                k_new = partial_rope(kv[..., :QK_DIM].reshape(N, T, N_HEADS, 2, HEAD_DIM), pos)
                v_new = kv[..., QK_DIM:].reshape(N, T, N_HEADS, V_DIM)
                if past_k is None:
                    k_all, v_all = k_new, v_new
                else:
                    k_all = jnp.concatenate([past_k.astype(k_new.dtype), k_new], axis=1)
                    v_all = jnp.concatenate([past_v.astype(v_new.dtype), v_new], axis=1)
                    k_pos = jnp.arange(k_all.shape[1])
            i = l - N_A_LAYERS
            lam_init = 0.8 - 0.6 * math.exp(-0.3 * l)
            lam = (jnp.exp(jnp.sum(lambda_q1[i].astype(jnp.float32) * lambda_k1[i].astype(jnp.float32)))
                   - jnp.exp(jnp.sum(lambda_q2[i].astype(jnp.float32) * lambda_k2[i].astype(jnp.float32)))
                   + lam_init)
            q = partial_rope((x @ w_q[i]).reshape(N, T, N_HEADS, 2, HEAD_DIM), pos)
            if past_k is None:
                o = blocked_causal_diff_attention(q, k_all, v_all, lam)
            else:
                o = diff_softmax_attention(q, k_all, v_all, pos, k_pos, lam)
            o = rms_norm(o, subln_g[i]) * (1.0 - lam_init)
            y = o.reshape(N, T, VO_DIM) @ w_o[i]
        x = layer_norm(DEEPNORM_ALPHA * x + y, ln1_g[l], ln1_b[l])
        x = layer_norm(DEEPNORM_ALPHA * x + swiglu(x, w_gate[l], w_up[l], w_down[l]), ln2_g[l], ln2_b[l])
    return x, k_new, v_new, jnp.stack(new_conv)


def setup_inputs(seed: int = 0) -> dict:
    key = jax.random.key(seed)
    ks = jax.random.split(key, 32)
    f32 = jnp.float32
    n_pages = PAST_LEN // PAGE_SIZE
    n_used = DEC_BATCH * n_pages
    n_pool = n_used + max(1, n_used // 4)
    nrm = lambda k, shape, s: jax.random.normal(k, shape, f32) * s

    x_prompt = nrm(ks[0], (BATCH, SEQ, D_MODEL), 1.0)
    x_sample = nrm(ks[1], (DEC_BATCH, DEC_SEQ, D_MODEL), 1.0)
    cache_k = nrm(ks[2], (n_pool, PAGE_SIZE, N_HEADS, 2, HEAD_DIM), 1.0)
    cache_v = nrm(ks[3], (n_pool, PAGE_SIZE, N_HEADS, V_DIM), DEEPNORM_BETA)
    state_conv = nrm(ks[4], (N_A_LAYERS, DEC_BATCH, CONV_W - 1, D_MODEL), 0.5)
    page_table = jax.random.permutation(ks[5], n_pool)[:n_used].reshape(DEC_BATCH, n_pages).astype(jnp.int32)

    sd = D_MODEL ** -0.5
    w_in = nrm(ks[6], (N_A_LAYERS, D_MODEL, 3 * D_MODEL), sd)
    w_in = w_in * jnp.concatenate([jnp.ones((2 * D_MODEL,), f32), jnp.full((D_MODEL,), DEEPNORM_BETA, f32)])
    conv_w = nrm(ks[7], (N_A_LAYERS, CONV_W, D_MODEL), CONV_W ** -0.5)
    w_mix_out = nrm(ks[8], (N_A_LAYERS, D_MODEL, D_MODEL), sd * DEEPNORM_BETA)
    w_kv = jnp.concatenate([nrm(ks[9], (D_MODEL, QK_DIM), sd),
                            nrm(ks[10], (D_MODEL, VO_DIM), sd * DEEPNORM_BETA)], axis=1)
    w_q = nrm(ks[11], (N_B_LAYERS, D_MODEL, QK_DIM), sd)
    w_o = nrm(ks[12], (N_B_LAYERS, VO_DIM, D_MODEL), VO_DIM ** -0.5 * DEEPNORM_BETA)
    lambda_q1 = nrm(ks[13], (N_B_LAYERS, HEAD_DIM), 0.1)
    lambda_k1 = nrm(ks[14], (N_B_LAYERS, HEAD_DIM), 0.1)
    lambda_q2 = nrm(ks[15], (N_B_LAYERS, HEAD_DIM), 0.1)
    lambda_k2 = nrm(ks[16], (N_B_LAYERS, HEAD_DIM), 0.1)
    subln_g = 1.0 + nrm(ks[17], (N_B_LAYERS, V_DIM), 0.02)
    ln1_g = 1.0 + nrm(ks[18], (DEPTH, D_MODEL), 0.02)
    ln1_b = nrm(ks[19], (DEPTH, D_MODEL), 0.02)
    w_gate = nrm(ks[20], (DEPTH, D_MODEL, D_FF), sd)
    w_up = nrm(ks[21], (DEPTH, D_MODEL, D_FF), sd * DEEPNORM_BETA)
    w_down = nrm(ks[22], (DEPTH, D_FF, D_MODEL), D_FF ** -0.5 * DEEPNORM_BETA)
    ln2_g = 1.0 + nrm(ks[23], (DEPTH, D_MODEL), 0.02)
    ln2_b = nrm(ks[24], (DEPTH, D_MODEL), 0.02)
    return {"x_prompt": x_prompt, "x_sample": x_sample, "cache_k": cache_k, "cache_v": cache_v,
            "state_conv": state_conv, "page_table": page_table,
            "w_in": w_in, "conv_w": conv_w, "w_mix_out": w_mix_out, "w_kv": w_kv,
            "w_q": w_q, "w_o": w_o, "lambda_q1": lambda_q1, "lambda_k1": lambda_k1,
            "lambda_q2": lambda_q2, "lambda_k2": lambda_k2, "subln_g": subln_g,
            "ln1_g": ln1_g, "ln1_b": ln1_b, "w_gate": w_gate, "w_up": w_up, "w_down": w_down,
            "ln2_g": ln2_g, "ln2_b": ln2_b}


def reference(x_prompt, x_sample, cache_k, cache_v, state_conv, page_table,
              w_in, conv_w, w_mix_out, w_kv, w_q, w_o,
              lambda_q1, lambda_k1, lambda_q2, lambda_k2, subln_g,
              ln1_g, ln1_b, w_gate, w_up, w_down, ln2_g, ln2_b):
    weights = (w_in, conv_w, w_mix_out, w_kv, w_q, w_o,
               lambda_q1, lambda_k1, lambda_q2, lambda_k2, subln_g,
               ln1_g, ln1_b, w_gate, w_up, w_down, ln2_g, ln2_b)
    n_p, t_p, _ = x_prompt.shape
    zero_conv = jnp.zeros((N_A_LAYERS, n_p, CONV_W - 1, D_MODEL), x_prompt.dtype)
    y_prompt, k_prompt, v_prompt, conv_prompt = trunk(
        x_prompt, jnp.arange(t_p), zero_conv, None, None, *weights)

    n_s, t_s, _ = x_sample.shape
    n_pages = page_table.shape[1]
    past_len = n_pages * cache_k.shape[1]
    past_k = cache_k[page_table].reshape(n_s, past_len, N_HEADS, 2, HEAD_DIM)
    past_v = cache_v[page_table].reshape(n_s, past_len, N_HEADS, V_DIM)
    y_sample, k_sample, v_sample, conv_sample = trunk(
        x_sample, past_len + jnp.arange(t_s), state_conv, past_k, past_v, *weights)
    return (y_prompt, y_sample, k_prompt, v_prompt, conv_prompt, k_sample, v_sample, conv_sample)
```

```python
import math
import numpy as np
from contextlib import ExitStack
import concourse.bass as bass
import concourse.mybir as mybir
from concourse.bass_utils import run_bass_kernel_spmd

F32 = mybir.dt.float32
BF16 = mybir.dt.bfloat16
I32 = mybir.dt.int32
ALU = mybir.AluOpType
AF = mybir.ActivationFunctionType
AX = mybir.AxisListType

D = 1024
KC = 8
DFF = 2816
NFF = 22
NB = 16
NT = 2240
SAMP0 = 2048
HALO0 = 2176
NH = 8
ALPHA = 8.0 ** 0.25
EPS = 1e-5
ROPE_THETA = 500000.0
NW = 10
N_POOL = 2560
SCALE = 0.125
FF_GROUPS = [list(range(0, 8)), list(range(8, 16)), list(range(16, 22))]


class Prog:
    ENGS = ("pe", "act", "dve", "pool", "sp")

    def __init__(self):
        self.ops = []
        self.last_writer = {}
        self.readers = {}
        self.dma_sem_counts = {}

    def op(self, eng, fn, reads=(), writes=(), dma=None, inc=16):
        idx = len(self.ops)
        deps = set()
        for k in reads:
            w = self.last_writer.get(k)
            if w is not None:
                deps.add(w)
        for k in writes:
            w = self.last_writer.get(k)
            if w is not None:
                deps.add(w)
            for r in self.readers.get(k, ()):
                deps.add(r)
        for k in reads:
            self.readers.setdefault(k, []).append(idx)
        for k in writes:
            self.last_writer[k] = idx
            self.readers[k] = []
        deps.discard(idx)
        rec = dict(eng=eng, fn=fn, deps=deps, dma=dma, inc=inc, has_cons=False, count=None)
        if dma is not None:
            c = self.dma_sem_counts.get(dma, 0) + inc
            self.dma_sem_counts[dma] = c
            rec["count"] = c
        self.ops.append(rec)
        return idx

    def emit(self, nc, stack):
        ops = self.ops
        for o in ops:
            for d in o["deps"]:
                p = ops[d]
                if p["eng"] == "pe" and o["eng"] == "pe" and p["dma"] is None:
                    continue
                p["has_cons"] = True
        esem = {e: stack.enter_context(nc.semaphore("es_" + e)) for e in self.ENGS}
        dsem = {k: stack.enter_context(nc.semaphore("ds_%d" % i)) for i, k in enumerate(self.dma_sem_counts)}
        cnt = {e: 0 for e in self.ENGS}
        for o in ops:
            if o["dma"] is None and o["has_cons"]:
                cnt[o["eng"]] += 1
                o["count"] = cnt[o["eng"]]
        per_eng = {e: [] for e in self.ENGS}
        running = {}
        for o in ops:
            waits = {}
            for d in o["deps"]:
                p = ops[d]
                if p["dma"] is not None:
                    key = ("d", p["dma"])
                    val = running[p["dma"]]
                else:
                    if p["eng"] == "pe" and o["eng"] == "pe":
                        continue
                    key = ("e", p["eng"])
                    val = p["count"]
                if val > waits.get(key, 0):
                    waits[key] = val
            if o["dma"] is not None:
                running[o["dma"]] = o["count"]
            per_eng[o["eng"]].append((o, waits))
        final_waits = {}
        for o in ops:
            if o["dma"] is not None:
                final_waits.setdefault(o["eng"], {})[o["dma"]] = self.dma_sem_counts[o["dma"]]
        block = stack.enter_context(nc.Block())

        def make(engname):
            def body(eng):
                known = {}
                for o, waits in per_eng[engname]:
                    for key, val in waits.items():
                        if known.get(key, 0) >= val:
                            continue
                        sem = dsem[key[1]] if key[0] == "d" else esem[key[1]]
                        eng.wait_ge(sem, val)
                        known[key] = val
                    ins = o["fn"](eng)
                    if o["dma"] is not None:
                        if o["inc"] == 16:
                            ins.then_inc(dsem[o["dma"]], 16)
                        else:
                            ins.then_inc(dsem[o["dma"]])
                    elif o["has_cons"]:
                        ins.then_inc(esem[engname], 1)
                for k, c in final_waits.get(engname, {}).items():
                    eng.wait_ge(dsem[k], c)
            return body

        block.tensor(make("pe"))
        block.scalar(make("act"))
        block.vector(make("dve"))
        block.gpsimd(make("pool"))
        block.sync(make("sp"))


def blocks_of(c0, n):
    return list(range(c0 // 128, (c0 + n - 1) // 128 + 1))


def build_program(n_layers_b=2):
    nc = bass.Bass("TRN2", target_bir_lowering=False)
    P = Prog()

    def din(name, shape, dt=F32):
        return nc.dram_tensor(name, shape, dt, kind="ExternalInput").ap()

    def dout(name, shape, dt=F32):
        return nc.dram_tensor(name, shape, dt, kind="ExternalOutput").ap()

    xp = din("xp", [2048, D]); xs = din("xs", [128, D]); xh = din("xh", [64, D])
    hv = din("hv", [128, 64]); posd = din("pos", [128, 18]); maskd = din("masks", [128, 512])
    pmod = din("pmod", [128, 1])
    w_in = din("w_in", [2, D, 3 * D]); cwd = din("cw", [128, 48]); w_mo = din("w_mix_out", [2, D, D])
    w_kv = din("w_kv", [D, 2 * D]); w_q = din("w_q", [2, D, D]); w_o = din("w_o", [2, D, D])
    w_gate = din("w_gate", [4, D, DFF]); w_up = din("w_up", [4, D, DFF]); w_down = din("w_down", [4, DFF, D])
    ln1_g = din("ln1_g", [4, D]); ln1_b = din("ln1_b", [4, D]); ln2_g = din("ln2_g", [4, D]); ln2_b = din("ln2_b", [4, D])
    subg = din("subln_g", [2, 128]); lamd = din("lam", [1, 512])
    wq_h = din("wq_h", [2, D, 128]); wkv_h = din("wkv_h", [D, 256])
    ckd = din("ck", [N_POOL * 8, 2048]); cvd = din("cv", [N_POOL * 8, 2048])
    ptrep = din("ptrep", [128, 128], I32)
    stc = din("state_conv", [2, 128, 2, D])

    yp = dout("yp", [2048, D]); ys = dout("ys", [128, D])
    kp = dout("kp", [2048, D]); vp = dout("vp", [2048, D])
    cpo = dout("cpo", [2, 2, D]); ks = dout("ks", [128, D]); vs = dout("vs", [128, D])
    cso = dout("cso", [2, 128, 2, D])

    kvb = [nc.dram_tensor("kvb%d" % h, [256, 2048], BF16).ap() for h in range(NH)]
    kvg = [nc.dram_tensor("kvg%d" % h, [1024, 2048], BF16).ap() for h in range(NH)]
    obd = [nc.dram_tensor("ob%d" % i, [128, 128], BF16).ap() for i in range(2)]
    ogd = [nc.dram_tensor("og%d" % i, [1024, 128], BF16).ap() for i in range(2)]
    og4d = [nc.dram_tensor("og4_%d" % i, [512, 128], BF16).ap() for i in range(2)]
    qhd = [nc.dram_tensor("qhd%d" % i, [128, 128], F32).ap() for i in range(2)]
    xsp = nc.dram_tensor("xsp", [1024, D], F32).ap()

    st = ExitStack()
    with st:
        def sb(name, shape, dt):
            return st.enter_context(nc.sbuf_tensor(name, shape, dt))

        X = sb("X", [128, 18, D], F32)
        XT = sb("XT", [128, 8, NT], BF16)
        R1 = sb("R1", [128, 8, NT], BF16)
        R2 = sb("R2", [128, 8352], F32)
        Wt = sb("Wt", [128, NW, 1024], BF16)
        identf = sb("identf", [128, 128], F32)
        onesf = sb("onesf", [128, 128], F32)
        onesb = sb("onesb", [128, 128], BF16)
        cw = sb("cwt", [128, 48], F32)
        hvt = sb("hvt", [128, 64], F32)
        post = sb("post", [128, 18], F32)
        maskb = sb("maskb", [128, 4, 128], BF16)
        CS = sb("CS", [128, 18, 16], F32)
        inv8 = sb("inv8", [128, 8], F32)
        SM = sb("SM", [128, 64], F32)
        lamt = sb("lamt", [128, 512], F32)
        gcol = sb("gcol", [128, 2], F32)
        pmt = sb("pmt", [128, 1], F32)
        ptt = sb("ptt", [128, 128], I32)
        IDX = sb("IDX", [128, 128], I32)
        QH = sb("QH", [128, 128], F32)
        KH = sb("KH", [128, 128], F32)
        VH = sb("VH", [128, 128], F32)
        VHb = sb("VHb", [128, 128], BF16)
        CPS = sb("CPS", [128, 2, 128], F32)
        ps = st.enter_context(nc.psum_tensor("ps", [128, 8, 512], F32))

        TT = R2[:, 0:288].rearrange("p (a b) -> p a b", a=18)
        TF = R2[:, 288:576].rearrange("p (a b) -> p a b", a=18)
        TI = R2[:, 576:864].bitcast(I32).rearrange("p (a b) -> p a b", a=18)
        R1f = R1[:, :, :].rearrange("p a b -> p (a b)")
        R2b = R2[:, :].bitcast(BF16)

        bank_ctr = [0]

        def nb():
            b = bank_ctr[0] % 8
            bank_ctr[0] += 1
            return b

        def dma(eng, out, in_, reads, writes, key):
            return P.op(eng, lambda e: e.dma_start(out=out, in_=in_), reads=reads, writes=writes, dma=key)

        def mm(out, lhsT, rhs, start, stop, reads, writes):
            return P.op("pe", lambda e: e.matmul(out, lhsT=lhsT, rhs=rhs, start=start, stop=stop), reads=reads, writes=writes)

        def tr(out, in_, rows, reads, writes):
            return P.op("pe", lambda e: e.transpose(out=out, in_=in_, identity=identf[0:rows, 0:rows]), reads=list(reads) + ["identf"], writes=writes)

        def act(out, in_, func, reads, writes, bias=None, scale=None, accum_out=None):
            kw = {}
            if bias is not None:
                kw["bias"] = bias
            if scale is not None:
                kw["scale"] = scale
            if accum_out is not None:
                kw["accum_out"] = accum_out
            return P.op("act", lambda e: e.activation(out=out, in_=in_, func=func, **kw), reads=reads, writes=writes)

        def tt(eng, out, in0, in1, op, reads, writes):
            return P.op(eng, lambda e: e.tensor_tensor(out=out, in0=in0, in1=in1, op=op), reads=reads, writes=writes)

        def ts(eng, out, in0, s1, op0, reads, writes, s2=None, op1=None):
            if op1 is None:
                return P.op(eng, lambda e: e.tensor_scalar(out=out, in0=in0, scalar1=s1, scalar2=None, op0=op0), reads=reads, writes=writes)
            return P.op(eng, lambda e: e.tensor_scalar(out=out, in0=in0, scalar1=s1, scalar2=s2, op0=op0, op1=op1), reads=reads, writes=writes)

        def stt(eng, out, in0, scalar, in1, op0, op1, reads, writes):
            return P.op(eng, lambda e: e.scalar_tensor_tensor(out=out, in0=in0, scalar=scalar, in1=in1, op0=op0, op1=op1), reads=reads, writes=writes)

        def cp(eng, out, in_, reads, writes):
            return P.op(eng, lambda e: e.tensor_copy(out=out, in_=in_), reads=reads, writes=writes)

        R2KEYS = ["Uext", "Umisc", "T0", ("hsb", 0), ("hsb", 1), ("STin", 0), ("STin", 1), ("stg", 0), ("stg", 1), "GB",
                  ("ssb", 0), ("ssb", 1), ("Ksb", 0), ("Ksb", 1), "rt0", "rt1", "rt2", "rt3", "rt0b", ("KP", 0), ("KP", 1),
                  ("VP", 0), ("VP", 1), ("EB", 0), ("EB", 1), ("EB", 2), ("EB", 3), "RZ", "EACC", ("OO", 0), "TT", "TI", "TF", "IDXF"]

        def fence():
            P.op("dve", lambda e: e.memset(SM[:, 40:41], 0.0), reads=[], writes=list(R2KEYS))

        wplan = []
        wstate = dict(next_load=0, next_use=0)

        def w_request(kind, src):
            wplan.append((kind, src))
            return len(wplan) - 1

        def w_emit_load(t):
            kind, src = wplan[t]
            s = t % NW
            if kind == "A":
                dst = Wt[:, s, :].rearrange("p (k o) -> p k o", k=8)
            else:
                dst = Wt[:, s, :]
            dma("pool", dst, src, reads=[], writes=[("W", s)], key=("w", s))

        def w_slot(t):
            while wstate["next_load"] <= t:
                w_emit_load(wstate["next_load"])
                wstate["next_load"] += 1
            return t % NW

        def w_done(t):
            lim = min(t + NW, len(wplan) - 1)
            while wstate["next_load"] <= lim:
                w_emit_load(wstate["next_load"])
                wstate["next_load"] += 1

        def plan_all():
            for l in range(2):
                for f in range(8):
                    for part in (1, 2, 0):
                        c0 = part * D + f * 128
                        w_request("A", w_in[l].rearrange("(k p) o -> p k o", p=128)[:, :, c0:c0 + 128])
                for f in range(8):
                    w_request("B", w_mo[l][f * 128:(f + 1) * 128, :])
                plan_ffn(l)
            for half in range(2):
                for kc in range(8):
                    w_request("B", w_kv[kc * 128:(kc + 1) * 128, half * D:(half + 1) * D])
            for i in range(n_layers_b):
                w_request("A", wq_h[i].rearrange("(k p) o -> p k o", p=128))
                if i == 0:
                    w_request("A", wkv_h.rearrange("(k p) o -> p k o", p=128)[:, :, 0:128])
                    w_request("A", wkv_h.rearrange("(k p) o -> p k o", p=128)[:, :, 128:256])
                for kc in range(8):
                    w_request("B", w_q[i][kc * 128:(kc + 1) * 128, :])
                for h in range(8):
                    w_request("B", w_o[i][h * 128:(h + 1) * 128, :])
                plan_ffn(2 + i)

        def plan_ffn(l):
            for grp in FF_GROUPS:
                for c in grp:
                    w_request("A", w_gate[l].rearrange("(k p) o -> p k o", p=128)[:, :, c * 128:(c + 1) * 128])
                    w_request("A", w_up[l].rearrange("(k p) o -> p k o", p=128)[:, :, c * 128:(c + 1) * 128])
                for c in grp:
                    w_request("B", w_down[l][c * 128:(c + 1) * 128, :])

        plan_all()
        wcur = [0]

        def w_next():
            t = wcur[0]
            wcur[0] += 1
            return t, w_slot(t)

        P.op("pool", lambda e: e.memset(identf[:, :], 0.0), writes=["identf"])
        P.op("pool", lambda e: e.affine_select(out=identf[:, :], in_=identf[:, :], compare_op=ALU.not_equal, fill=1.0,
                                               base=0, pattern=[[-1, 128]], channel_multiplier=1), reads=["identf"], writes=["identf"])
        P.op("pool", lambda e: e.memset(onesf[:, :], 1.0), writes=["onesf"])
        P.op("pool", lambda e: e.memset(onesb[:, :], 1.0), writes=["onesb"])
        for i in range(8):
            v = float(ROPE_THETA ** (-(2.0 * i) / 16.0))
            P.op("pool", lambda e, i=i, v=v: e.memset(inv8[:, i:i + 1], v), writes=["inv8"])
        P.op("pool", lambda e: e.memset(SM[:, 16:17], EPS), writes=["eps"])
        P.op("pool", lambda e: e.memset(SM[:, 17:18], 128.0 * EPS), writes=["eps"])
        dma("sp", cw[:, :], cwd, [], ["cw"], "c0")
        dma("sp", hvt[:, :], hv, [], ["hvt"], "c0")
        dma("sp", post[:, :], posd, [], ["post"], "c0")
        dma("sp", pmt[:, :], pmod, [], ["pmt"], "c0")
        dma("sp", ptt[:, :], ptrep, [], ["ptt"], "c0")
        dma("sp", lamt[:, :], lamd.partition_broadcast(128), [], ["lamt"], "c0")
        dma("pool", maskb[:, :, :], maskd.rearrange("p (r q) -> p r q", r=4), [], ["maskb"], "c1")
        for i in range(2):
            dma("sp", gcol[:, i:i + 1], subg[i:i + 1, :].rearrange("o v -> v o"), [], ["gcol"], "c0")
        for l in range(NB):
            dma("sp", X[:, l, :], xp[l * 128:(l + 1) * 128, :], [], [("X", l)], ("x", l % 4))
        dma("sp", X[:, 16, :], xs, [], [("X", 16)], ("x", 0))
        dma("sp", X[0:64, 17, :], xh, [], [("X", 17)], ("x", 1))

        tt("dve", TT[:, :, 8:16], post[:, :].unsqueeze(2).broadcast_to([128, 18, 8]),
           inv8[:, :].unsqueeze(1).broadcast_to([128, 18, 8]), ALU.mult, ["post", "inv8"], ["TT"])
        ts("dve", TT[:, :, 8:16], TT[:, :, 8:16], float(1.0 / (2 * math.pi)), ALU.mult, ["TT"], ["TT"])
        ts("dve", TT[:, :, 0:8], TT[:, :, 8:16], 0.25, ALU.add, ["TT"], ["TT"])
        cp("dve", TI[:, :, :], TT[:, :, :], ["TT"], ["TI"])
        cp("dve", TF[:, :, :], TI[:, :, :], ["TI"], ["TF"])
        tt("dve", TT[:, :, :], TT[:, :, :], TF[:, :, :], ALU.subtract, ["TT", "TF"], ["TT"])
        stt("dve", TF[:, :, :], TT[:, :, :], 0.5, TT[:, :, :], ALU.is_gt, ALU.subtract, ["TT"], ["TF"])
        stt("dve", TT[:, :, :], TF[:, :, :], 0.5, TF[:, :, :], ALU.is_gt, ALU.subtract, ["TF"], ["TT"])
        act(CS[:, :, :], TT[:, :, :], AF.Sin, ["TT"], ["CS"], scale=float(2 * math.pi))

        IDXF = R2[:, 1024:1152]
        ts("dve", IDXF, ptt[:, :], 8.0, ALU.mult, ["ptt"], ["IDXF"], s2=pmt[:, 0:1], op1=ALU.add)
        cp("dve", IDX[:, :], IDXF, ["IDXF"], ["IDX"])

        def rows_of(blk):
            return 64 if blk == 17 else 128

        def col0_of(blk):
            return blk * 128 if blk < 17 else HALO0

        def make_XT(blk):
            rows = rows_of(blk)
            c0 = col0_of(blk)
            for half in range(2):
                b = nb()
                for k in range(4):
                    kk = half * 4 + k
                    tr(ps[:, b, k * 128:k * 128 + rows], X[0:rows, blk, kk * 128:(kk + 1) * 128], rows,
                       [("X", blk)], [("ps", b)])
                src = ps[:, b, :].rearrange("p (k n) -> p k n", k=4)[:, :, 0:rows]
                dst = XT[:, half * 4:(half + 1) * 4, c0:c0 + rows]
                if (blk + half) % 2 == 0:
                    act(dst, src, AF.Copy, [("ps", b)], [("XT", blk)])
                else:
                    cp("dve", dst, src, [("ps", b)], [("XT", blk)])

        GBv = R2[:, 6304:8352]

        def load_gb(gd, bd, l):
            dma("sp", GBv[:, 0:1024], gd[l:l + 1, :].partition_broadcast(128), [], ["GB"], "gb")
            dma("sp", GBv[:, 1024:2048], bd[l:l + 1, :].partition_broadcast(128), [], ["GB"], "gb")

        def layer_norm(blk):
            rows = rows_of(blk)
            xk = ("X", blk)
            xb = X[0:rows, blk, :]
            P.op("dve", lambda e: e.bn_stats(out=SM[0:rows, 0:6], in_=X[0:rows, blk, 0:512]), reads=[xk], writes=["st0"])
            P.op("dve", lambda e: e.bn_stats(out=SM[0:rows, 6:12], in_=X[0:rows, blk, 512:1024]), reads=[xk], writes=["st1"])
            P.op("dve", lambda e: e.bn_aggr(out=SM[0:rows, 12:14], in_=SM[0:rows, 0:12]), reads=["st0", "st1"], writes=["mv"])
            act(SM[0:rows, 14:15], SM[0:rows, 13:14], AF.Ln, ["mv", "eps"], ["rstd"], bias=SM[0:rows, 16:17], scale=1.0)
            act(SM[0:rows, 14:15], SM[0:rows, 14:15], AF.Exp, ["rstd"], ["rstd"], scale=-0.5)
            stt("dve", SM[0:rows, 15:16], SM[0:rows, 12:13], -1.0, SM[0:rows, 14:15], ALU.mult, ALU.mult, ["mv", "rstd"], ["nmr"])
            act(xb, xb, AF.Identity, [xk, "rstd", "nmr"], [xk], bias=SM[0:rows, 15:16], scale=SM[0:rows, 14:15])
            tt("dve", xb, xb, GBv[0:rows, 0:1024], ALU.mult, [xk, "GB"], [xk])
            tt("dve", xb, xb, GBv[0:rows, 1024:2048], ALU.add, [xk, "GB"], [xk])

        def resid_accum(blk, b0, b1, first):
            rows = rows_of(blk)
            xk = ("X", blk)
            for half, b in ((0, b0), (1, b1)):
                xs_ = X[0:rows, blk, half * 512:(half + 1) * 512]
                if first:
                    stt("dve", xs_, xs_, ALPHA, ps[0:rows, b, :], ALU.mult, ALU.add, [xk, ("ps", b)], [xk])
                else:
                    tt("dve", xs_, xs_, ps[0:rows, b, :], ALU.add, [xk, ("ps", b)], [xk])

        def proj_tokmajor(blk, src_keys_fn, lhs_fn, nk, wslots, first=True):
            rows = rows_of(blk)
            b0, b1 = nb(), nb()
            for half, b in ((0, b0), (1, b1)):
                for k in range(nk):
                    mm(ps[0:rows, b, :], lhs_fn(k), Wt[:, wslots[k], half * 512:(half + 1) * 512], k == 0, k == nk - 1,
                       list(src_keys_fn(k)) + [("W", wslots[k])], [("ps", b)])
            return b0, b1

        def ffn(l, blocks, tiles, last_layer):
            fence()
            load_gb(ln2_g, ln2_b, l)
            for gi, grp in enumerate(FF_GROUPS):
                for ci, c in enumerate(grp):
                    tg, sg = w_next()
                    tu, su = w_next()
                    for (c0, n) in tiles:
                        bg, bu = nb(), nb()
                        xkeys = [("XT", b) for b in blocks_of(c0, n)]
                        for k in range(8):
                            mm(ps[:, bg, 0:n], Wt[:, sg, k * 128:(k + 1) * 128], XT[:, k, c0:c0 + n], k == 0, k == 7,
                               xkeys + [("W", sg)], [("ps", bg)])
                        for k in range(8):
                            mm(ps[:, bu, 0:n], Wt[:, su, k * 128:(k + 1) * 128], XT[:, k, c0:c0 + n], k == 0, k == 7,
                               xkeys + [("W", su)], [("ps", bu)])
                        slot = bank_ctr[0] % 2
                        ssb = R2[:, slot * 512:slot * 512 + n]
                        act(ssb, ps[:, bg, 0:n], AF.Silu, [("ps", bg)], [("ssb", slot)])
                        tt("dve", R1[:, ci, c0:c0 + n], ps[:, bu, 0:n], ssb, ALU.mult, [("ps", bu), ("ssb", slot)],
                           [("R1", ci, b) for b in blocks_of(c0, n)])
                    w_done(tu)
                dts = [w_next() for _ in grp]
                dslots = [s for (_, s) in dts]
                for blk in blocks:
                    c0 = col0_of(blk)
                    rows = rows_of(blk)
                    b0, b1 = proj_tokmajor(blk, lambda k: [("R1", k, blk)], lambda k: R1[:, k, c0:c0 + rows], len(grp), dslots)
                    resid_accum(blk, b0, b1, first=(gi == 0))
                    if gi == len(FF_GROUPS) - 1:
                        layer_norm(blk)
                        if not last_layer:
                            make_XT(blk)
                w_done(dts[-1][0])

        for blk in range(18):
            make_XT(blk)

        TILES_A = [(2048, 192), (0, 512), (512, 512), (1024, 512), (1536, 512)]
        TILES_B = [(2048, 128), (0, 512), (512, 512), (1024, 512), (1536, 512)]
        BLOCKS_A = list(range(18))
        BLOCKS_B = list(range(17))

        Uext = R2[:, 0:2080].rearrange("p (l t) -> p l t", l=16)
        Umisc = R2[:, 2080:2272]
        T0 = R2[:, 2272:4512]
        T0p = T0[:, 0:2048].rearrange("p (l t) -> p l t", l=16)
        T0h = T0[:, HALO0:NT].rearrange("p (l t) -> p l t", l=16)
        Uh = Umisc[:, 128:192].rearrange("p (l t) -> p l t", l=16)

        for l in range(2):
            fence()
            load_gb(ln1_g, ln1_b, l)
            dma("sp", cso[l, :, 0, :], stc[l, :, 1, :], [], [], "misc")
            for f in range(8):
                tc_, sc_ = w_next()
                th_, sh_ = w_next()
                tb_, sb_ = w_next()
                sts = f % 2
                STin = R2[:, 5536 + sts * 256:5536 + (sts + 1) * 256].rearrange("p (a b) -> p a b", a=2)
                dma("sp", STin, stc[l, :, :, f * 128:(f + 1) * 128], [], [("STin", sts)], ("st", sts))
                for (c0, n) in TILES_A:
                    bc, bh = nb(), nb()
                    xkeys = [("XT", b) for b in blocks_of(c0, n)]
                    for k in range(8):
                        mm(ps[:, bc, 0:n], Wt[:, sc_, k * 128:(k + 1) * 128], XT[:, k, c0:c0 + n], k == 0, k == 7,
                           xkeys + [("W", sc_)], [("ps", bc)])
                    for k in range(8):
                        mm(ps[:, bh, 0:n], Wt[:, sh_, k * 128:(k + 1) * 128], XT[:, k, c0:c0 + n], k == 0, k == 7,
                           xkeys + [("W", sh_)], [("ps", bh)])
                    hs = bank_ctr[0] % 2
                    hsb = R2[:, 4512 + hs * 512:4512 + hs * 512 + n]
                    act(hsb, ps[:, bh, 0:n], AF.Copy, [("ps", bh)], [("hsb", hs)])
                    if c0 == SAMP0:
                        tt("dve", Umisc[:, 0:n], ps[:, bc, 0:n], hsb, ALU.mult, [("ps", bc), ("hsb", hs)], ["Umisc"])
                        tt("dve", Umisc[:, 128:192], Umisc[:, 128:192], hvt[:, :], ALU.mult, ["Umisc", "hvt"], ["Umisc"])
                        cp("dve", Uext[:, :, 0:2], Uh[:, :, 2:4], ["Umisc"], ["Uext"])
                    else:
                        l0 = c0 // 128
                        tt("dve", Uext[:, l0:l0 + 4, 2:130], ps[:, bc, 0:n].rearrange("p (l t) -> p l t", l=4),
                           hsb.rearrange("p (l t) -> p l t", l=4), ALU.mult, [("ps", bc), ("hsb", hs)], ["Uext"])
                w_done(th_)
                wc = lambda tap: cw[:, (l * 3 + tap) * 8 + f:(l * 3 + tap) * 8 + f + 1]
                ts("dve", T0p, Uext[:, :, 2:130], wc(2), ALU.mult, ["Uext", "cw"], ["T0"])
                stt("dve", T0p, Uext[:, :, 1:129], wc(1), T0p, ALU.mult, ALU.add, ["Uext", "T0", "cw"], ["T0"])
                stt("dve", T0p, Uext[:, :, 0:128], wc(0), T0p, ALU.mult, ALU.add, ["Uext", "T0", "cw"], ["T0"])
                bs = nb()
                for a in range(2):
                    tr(ps[:, bs, a * 128:(a + 1) * 128], STin[:, a, :], 128, [("STin", sts)], [("ps", bs)])
                T0s = T0[:, SAMP0:SAMP0 + 128]
                ts("dve", T0s, Umisc[:, 0:128], wc(2), ALU.mult, ["Umisc", "cw"], ["T0"])
                stt("dve", T0s, ps[:, bs, 128:256], wc(1), T0s, ALU.mult, ALU.add, [("ps", bs), "T0", "cw"], ["T0"])
                stt("dve", T0s, ps[:, bs, 0:128], wc(0), T0s, ALU.mult, ALU.add, [("ps", bs), "T0", "cw"], ["T0"])
                ts("dve", T0h, Uh, wc(2), ALU.mult, ["Umisc", "cw"], ["T0"])
                stt("dve", T0h[:, :, 2:4], Uh[:, :, 1:3], wc(1), T0h[:, :, 2:4], ALU.mult, ALU.add, ["Umisc", "T0", "cw"], ["T0"])
                stt("dve", T0h[:, :, 2:4], Uh[:, :, 0:2], wc(0), T0h[:, :, 2:4], ALU.mult, ALU.add, ["Umisc", "T0", "cw"], ["T0"])
                bo = nb()
                tr(ps[:, bo, 0:128], Umisc[:, 0:128], 128, ["Umisc"], [("ps", bo)])
                tr(ps[0:2, bo, 128:256], Uext[:, 15, 128:130], 128, ["Uext"], [("ps", bo)])
                sg = f % 2
                stg = R2[:, 6048 + sg * 128:6048 + (sg + 1) * 128]
                act(stg, ps[:, bo, 0:128], AF.Copy, [("ps", bo)], [("stg", sg)])
                dma("sp", cso[l, :, 1, f * 128:(f + 1) * 128], stg, [("stg", sg)], [], ("so", sg))
                act(CPS[0:2, sg, :], ps[0:2, bo, 128:256], AF.Copy, [("ps", bo)], [("cps", sg)])
                dma("sp", cpo[l, :, f * 128:(f + 1) * 128], CPS[0:2, sg, :], [("cps", sg)], [], ("so", sg))
                for (c0, n) in TILES_A:
                    bb = nb()
                    xkeys = [("XT", b) for b in blocks_of(c0, n)]
                    for k in range(8):
                        mm(ps[:, bb, 0:n], Wt[:, sb_, k * 128:(k + 1) * 128], XT[:, k, c0:c0 + n], k == 0, k == 7,
                           xkeys + [("W", sb_)], [("ps", bb)])
                    tt("dve", R1[:, f, c0:c0 + n], ps[:, bb, 0:n], T0[:, c0:c0 + n], ALU.mult, [("ps", bb), "T0"],
                       [("R1", f, b) for b in blocks_of(c0, n)])
                w_done(tb_)
            ots = [w_next() for _ in range(8)]
            oslots = [s for (_, s) in ots]
            for blk in BLOCKS_A:
                c0 = col0_of(blk)
                rows = rows_of(blk)
                b0, b1 = proj_tokmajor(blk, lambda k: [("R1", k, blk)], lambda k: R1[:, k, c0:c0 + rows], 8, oslots)
                resid_accum(blk, b0, b1, first=True)
                layer_norm(blk)
                make_XT(blk)
            w_done(ots[-1][0])
            ffn(l, BLOCKS_A, TILES_A, last_layer=False)

        Ksb = [R2[:, 0:1024], R2[:, 1024:2048]]
        RT = R2[:, 2048:2560]

        def rope(eng, T, groups, blk, rows, key):
            Tv = T.rearrange("p (g d) -> p g d", g=groups)
            x1 = Tv[:, :, 0:8]
            x2 = Tv[:, :, 8:16]
            cosb = CS[0:rows, blk, 0:8].unsqueeze(1).broadcast_to([rows, groups, 8])
            sinb = CS[0:rows, blk, 8:16].unsqueeze(1).broadcast_to([rows, groups, 8])
            t = [RT[0:rows, i * 128:i * 128 + groups * 8].rearrange("p (g d) -> p g d", g=groups) for i in range(4)]
            tt(eng, t[0], x1, cosb, ALU.mult, [key, "CS"], ["rt0"])
            tt(eng, t[1], x2, sinb, ALU.mult, [key, "CS"], ["rt1"])
            tt(eng, t[2], x2, cosb, ALU.mult, [key, "CS"], ["rt2"])
            tt(eng, t[3], x1, sinb, ALU.mult, [key, "CS"], ["rt3"])
            tt(eng, x1, t[0], t[1], ALU.subtract, ["rt0", "rt1", key], [key])
            tt(eng, x2, t[2], t[3], ALU.add, ["rt2", "rt3", key], [key])

        R1ALL = [("R1", a, b) for a in range(8) for b in range(18)]
        Vbf = R1f[:, 0:16384].rearrange("p (l v) -> p l v", l=16)

        fence()
        for half in range(2):
            kts = [w_next() for _ in range(8)]
            kslots = [s for (_, s) in kts]
            for blk in BLOCKS_B:
                c0 = col0_of(blk)
                b0, b1 = proj_tokmajor(blk, lambda k: [("XT", blk)], lambda k: XT[:, k, c0:c0 + 128], 8, kslots)
                sl = blk % 2
                kk = ("Ksb", sl)
                act(Ksb[sl][:, 0:512], ps[:, b0, :], AF.Copy, [("ps", b0)], [kk])
                act(Ksb[sl][:, 512:1024], ps[:, b1, :], AF.Copy, [("ps", b1)], [kk])
                if half == 0:
                    rope("dve", Ksb[sl], 16, blk, 128, kk)
                    dst = kp[blk * 128:(blk + 1) * 128, :] if blk < 16 else ks
                    dma("sp", dst, Ksb[sl], [kk], [], ("ko", sl))
                    if blk < 16:
                        for hh in range(2):
                            b = nb()
                            for k in range(4):
                                h = hh * 4 + k
                                tr(ps[:, b, k * 128:(k + 1) * 128], Ksb[sl][:, h * 128:(h + 1) * 128], 128, [kk], [("ps", b)])
                            src = ps[:, b, :].rearrange("p (k n) -> p k n", k=4)
                            dstT = R1[:, hh * 4:(hh + 1) * 4, c0:c0 + 128]
                            wk = [("R1", hh * 4 + k, blk) for k in range(4)]
                            if hh == 0:
                                act(dstT, src, AF.Copy, [("ps", b)], wk)
                            else:
                                cp("dve", dstT, src, [("ps", b)], wk)
                else:
                    dst = vp[blk * 128:(blk + 1) * 128, :] if blk < 16 else vs
                    dma("sp", dst, Ksb[sl], [kk], [], ("ko", sl))
                    if blk < 16:
                        cp("dve", Vbf[:, blk, :], Ksb[sl], [kk], R1ALL + [("Vbf", blk)])
            w_done(kts[-1][0])
            if half == 0:
                for h in range(NH):
                    dma("sp", kvb[h][0:128, :], R1[:, h, 0:2048], [("R1", h, b) for b in range(16)], [("kvb", h)], ("kb", h % 4))
            else:
                for h in range(NH):
                    dma("sp", kvb[h][128:256, :].rearrange("p (l v) -> p l v", l=16), Vbf[:, :, h * 128:(h + 1) * 128],
                        R1ALL + [("Vbf", b) for b in range(16)], [("kvb", h)], ("kb", h % 4))

        qsb = Ksb
        KP = [R2b[:, s * 2048:(s + 1) * 2048].rearrange("p (r k) -> p r k", r=4) for s in range(2)]
        VP = [R2b[:, 4096 + s * 2048:4096 + (s + 1) * 2048].rearrange("p (r k) -> p r k", r=4) for s in range(2)]
        EB = [R2b[:, 8192 + s * 1024:8192 + (s + 1) * 1024].rearrange("p (c q) -> p c q", c=2) for s in range(4)]
        EACC = R2[:, 6144:7168].rearrange("p (c q) -> p c q", c=2)
        RZ = R2[:, 7168:7680]
        OO = [R2[:, 7680:8192]]

        for i in range(n_layers_b):
            l = 2 + i
            lam_init = 0.8 - 0.6 * math.exp(-0.3 * l)
            lq1 = lamt[:, (0 * 2 + i) * 64:(0 * 2 + i + 1) * 64]
            lk1 = lamt[:, (1 * 2 + i) * 64:(1 * 2 + i + 1) * 64]
            lq2 = lamt[:, (2 * 2 + i) * 64:(2 * 2 + i + 1) * 64]
            lk2 = lamt[:, (3 * 2 + i) * 64:(3 * 2 + i + 1) * 64]
            tt("dve", RT[:, 0:64], lq1, lk1, ALU.mult, ["lamt", "rt0"], ["rt0"])
            P.op("dve", lambda e: e.tensor_reduce(out=SM[:, 24:25], in_=RT[:, 0:64], axis=AX.X, op=ALU.add), reads=["rt0"], writes=["lam1"])
            tt("dve", RT[:, 64:128], lq2, lk2, ALU.mult, ["lamt", "rt0"], ["rt0b"])
            P.op("dve", lambda e: e.tensor_reduce(out=SM[:, 25:26], in_=RT[:, 64:128], axis=AX.X, op=ALU.add), reads=["rt0b"], writes=["lam2"])
            act(SM[:, 24:26], SM[:, 24:26], AF.Exp, ["lam1", "lam2"], ["lame"])
            tt("dve", SM[:, 26:27], SM[:, 24:25], SM[:, 25:26], ALU.subtract, ["lame"], ["lam"])
            ts("dve", SM[:, 36 + i:37 + i], SM[:, 26:27], float(lam_init), ALU.add, ["lam"], [("nlam", i)], s2=-1.0, op1=ALU.mult)
            nlam = SM[:, 36 + i:37 + i]
            ts("dve", SM[:, 28 + i:29 + i], gcol[:, i:i + 1], float(math.sqrt(128.0) * (1.0 - lam_init)), ALU.mult, ["gcol"], [("gsc", i)])
            gsc = SM[:, 28 + i:29 + i]

            fence()
            tqh, sqh = w_next()
            bq = nb()
            for k in range(8):
                mm(ps[:, bq, 0:128], XT[:, k, SAMP0:SAMP0 + 128], Wt[:, sqh, k * 128:(k + 1) * 128], k == 0, k == 7,
                   [("XT", 16), ("W", sqh)], [("ps", bq)])
            act(QH[:, :], ps[:, bq, 0:128], AF.Copy, [("ps", bq)], ["QH"])
            rope("dve", QH[:, :], 2, 16, 128, "QH")
            if i == 0:
                tkh, skh = w_next()
                tvh, svh = w_next()
                bk = nb()
                for k in range(8):
                    mm(ps[:, bk, 0:128], XT[:, k, SAMP0:SAMP0 + 128], Wt[:, skh, k * 128:(k + 1) * 128], k == 0, k == 7,
                       [("XT", 16), ("W", skh)], [("ps", bk)])
                act(KH[:, :], ps[:, bk, 0:128], AF.Copy, [("ps", bk)], ["KH"])
                rope("dve", KH[:, :], 2, 16, 128, "KH")
                bv = nb()
                for k in range(8):
                    mm(ps[:, bv, 0:128], XT[:, k, SAMP0:SAMP0 + 128], Wt[:, svh, k * 128:(k + 1) * 128], k == 0, k == 7,
                       [("XT", 16), ("W", svh)], [("ps", bv)])
                act(VH[:, :], ps[:, bv, 0:128], AF.Copy, [("ps", bv)], ["VH"])
                cp("dve", VHb[:, :], VH[:, :], ["VH"], ["VHb"])
                w_done(tvh)
            else:
                w_done(tqh)
            qts = [w_next() for _ in range(8)]
            qslots = [s for (_, s) in qts]
            for blk in range(16):
                c0 = col0_of(blk)
                b0, b1 = proj_tokmajor(blk, lambda k: [("XT", blk)], lambda k: XT[:, k, c0:c0 + 128], 8, qslots)
                sl = blk % 2
                kk = ("Ksb", sl)
                act(qsb[sl][:, 0:512], ps[:, b0, :], AF.Copy, [("ps", b0)], [kk])
                act(qsb[sl][:, 512:1024], ps[:, b1, :], AF.Copy, [("ps", b1)], [kk])
                rope("dve", qsb[sl], 16, blk, 128, kk)
                for hh in range(2):
                    b = nb()
                    for k in range(4):
                        h = hh * 4 + k
                        tr(ps[:, b, k * 128:(k + 1) * 128], qsb[sl][:, h * 128:(h + 1) * 128], 128, [kk], [("ps", b)])
                    src = ps[:, b, :].rearrange("p (k n) -> p k n", k=4)
                    dstT = XT[:, hh * 4:(hh + 1) * 4, c0:c0 + 128]
                    if hh == 0:
                        act(dstT, src, AF.Copy, [("ps", b)], [("XT", blk)])
                    else:
                        cp("dve", dstT, src, [("ps", b)], [("XT", blk)])
            w_done(qts[-1][0])
            QT = XT

            if i == 0:
                for h in range(NH):
                    P.op("pool", lambda e, h=h: e.collective_compute("AllGather", ALU.bypass, replica_groups=[[0, 1, 2, 3], [4, 5, 6, 7]],
                                                                      ins=[kvb[h]], outs=[kvg[h]]),
                         reads=[("kvb", h)], writes=[("kvg", h)], dma=("cc", h), inc=1)
            fence()
            XS = X[:, 0:8, :].rearrange("p a b -> p (a b)")
            XSb = XS.bitcast(BF16)
            SAKEYS = [("KN", 0), ("KN", 1), ("VN", 0), ("VN", 1), "PR", "PRsq", "PRr", ("SS", 0), ("SS", 1), ("ENs", 0), ("ENs", 1), "DM", "OSB",
                      ("QBC", 0), ("QBC", 1), "OSsb"]
            for b in range(8):
                dma("sp", xsp[b * 128:(b + 1) * 128, :], X[:, b, :], [("X", b)], [("xsp", b)], ("xsp", b % 2))
            P.op("dve", lambda e: e.memset(SM[:, 41:42], 0.0), reads=[], writes=[("X", b) for b in range(8)] + SAKEYS)
            KN = [XSb[:, s * 2048:(s + 1) * 2048] for s in range(2)]
            VN = [XSb[:, 4096 + s * 2048:4096 + (s + 1) * 2048] for s in range(2)]
            PR = XS[:, 4096:6144]
            QBC = [XS[:, 6144 + s * 128:6144 + (s + 1) * 128] for s in range(2)]
            SS = XS[:, 6400:6432]
            EN = XSb[:, 12864:12896].rearrange("p (s c) -> p s c", c=2)
            ESUM = XS[:, 6448:6450]
            DM = XSb[:, 12928:13184].rearrange("p (c n) -> p c n", c=2)
            OSsb = XS[:, 6592:7104].rearrange("p (n k) -> p n k", k=4)
            FIN = XS[:, 7104:7616]
            OSB = XSb[:, 15232:15360]
            SAB = 7
            dma("sp", qhd[i], QH[:, :], ["QH"], [("qhd", i)], "qhd")
            tt("dve", PR[:, 0:128], QH[:, :], KH[:, :], ALU.mult, ["QH", "KH"], ["PR"])
            P.op("dve", lambda e: e.tensor_reduce(out=SM[:, 32:34], in_=PR[:, 0:128].rearrange("p (c d) -> p c d", c=2), axis=AX.X, op=ALU.add),
                 reads=["PR"], writes=["snew"])
            act(SM[:, 32:34], SM[:, 32:34], AF.Exp, ["snew"], ["enew"], scale=SCALE)
            for c in range(2):
                ts("dve", DM[:, c, :], identf[:, :], SM[:, 32 + c:33 + c], ALU.mult, ["identf", "enew"], ["DM"])

            def sa_gather_k(n):
                sl = n % 2
                P.op("pool", lambda e: e.indirect_dma_start(out=KN[sl], out_offset=None, in_=ckd,
                                                             in_offset=bass.IndirectOffsetOnAxis(ap=IDX[:, n:n + 1], axis=0)),
                     reads=["IDX"], writes=[("KN", sl)], dma=("gk", sl))
                dma("sp", QBC[sl], qhd[i][n:n + 1, :].partition_broadcast(128), [("qhd", i)], [("QBC", sl)], ("qbc", sl))

            def sa_gather_v(n):
                sl = n % 2
                P.op("pool", lambda e: e.indirect_dma_start(out=VN[sl], out_offset=None, in_=cvd,
                                                             in_offset=bass.IndirectOffsetOnAxis(ap=IDX[:, n:n + 1], axis=0)),
                     reads=["IDX"], writes=[("VN", sl)], dma=("gv", sl))

            ENs = [XSb[:, 12864 + s_ * 32:12896 + s_ * 32].rearrange("p (s c) -> p s c", c=2) for s_ in range(2)]
            ESUMs = [XS[:, 7680 + s_ * 2:7682 + s_ * 2] for s_ in range(2)]
            SSs = [XS[:, 6400:6432], XS[:, 7700:7732]]

            def sa_front(n):
                sl = n % 2
                if n == 0:
                    sa_gather_k(0)
                if n + 1 < 128:
                    sa_gather_k(n + 1)
                tt("dve", PR.rearrange("p (s f) -> p s f", s=16), KN[sl].rearrange("p (s f) -> p s f", s=16),
                   QBC[sl].unsqueeze(1).broadcast_to([128, 16, 128]), ALU.mult, [("KN", sl), ("QBC", sl)], ["PR"])
                P.op("dve", lambda e: e.tensor_reduce(out=SSs[sl], in_=PR.rearrange("p (g d) -> p g d", d=64), axis=AX.X, op=ALU.add),
                     reads=["PR"], writes=[("SS", sl)])

            def sa_mid(n):
                sl = n % 2
                sa_gather_v(n)
                SSv = SSs[sl].rearrange("p (s c) -> p s c", c=2)
                for c in range(2):
                    act(ENs[sl][:, :, c], SSv[:, :, c], AF.Exp, [("SS", sl)], [("ENs", sl)], scale=SCALE, accum_out=ESUMs[sl][:, c:c + 1])

            def sa_back(n):
                sl = n % 2
                mm(ps[:, SAB, 0:2], VHb[:, :], DM[:, :, n], True, False, ["VHb", "DM"], [("ps", SAB)])
                for s16 in range(16):
                    mm(ps[:, SAB, 0:2], VN[sl][:, s16 * 128:(s16 + 1) * 128], ENs[sl][:, s16, :], False, s16 == 15,
                       [("VN", sl), ("ENs", sl)], [("ps", SAB)])
                mm(ps[:, SAB, 2:4], onesf[:, :], ESUMs[sl], True, False, [("ENs", sl), "onesf"], [("ps", SAB)])
                mm(ps[:, SAB, 2:4], onesb[:, :], DM[:, :, n], False, True, ["onesb", "DM"], [("ps", SAB)])
                act(OSsb[:, n, :], ps[:, SAB, 0:4], AF.Copy, [("ps", SAB)], ["OSsb"])

            N_SLOTS = 130

            def sa_slot(k):
                if 2 <= k < 130:
                    sa_back(k - 2)
                if 1 <= k < 129:
                    sa_mid(k - 1)
                if k < 128:
                    sa_front(k)

            bank_ctr[0] = 0
            SB = [(0, 1), (2, 3)]
            OB = (4, 5)
            ZBK = 6
            step = 0
            piece = [0]
            sa_next = [0]
            for h in range(NH):
                for qc in range(4):
                    nkb = 16 * (qc + 1)
                    steps = []
                    for lp in range(qc + 1):
                        for a in range(4):
                            lk = lp * 4 + a
                            for r in range(4):
                                steps.append((lp, a, lk, r))
                    pslots = {}

                    def emit_qk(si):
                        lp, a, lk, r = steps[si]
                        if lp not in pslots:
                            pslot = piece[0] % 2
                            piece[0] += 1
                            pslots[lp] = pslot
                            ksrc = kvg[h].rearrange("(r x) c -> x r c", x=256)
                            dma("sp", KP[pslot], ksrc[0:128, :, lp * 512:(lp + 1) * 512], [("kvg", h)], [("KP", pslot)], ("kp", pslot))
                            dma("sp", VP[pslot], ksrc[128:256, :, lp * 512:(lp + 1) * 512], [("kvg", h)], [("VP", pslot)], ("vp", pslot))
                        pslot = pslots[lp]
                        off = max(0, lk - 4 * qc)
                        n = 512 - 128 * off
                        qc0 = (4 * qc + off) * 128
                        g = (step_base[0] + si)
                        sp_ = SB[g % 2]
                        es = g % 4
                        qkeys = [("XT", b) for b in range(4 * qc + off, 4 * qc + 4)]
                        for c in range(2):
                            mm(ps[:, sp_[c], 0:n], KP[pslot][c * 64:(c + 1) * 64, r, a * 128:(a + 1) * 128],
                               QT[c * 64:(c + 1) * 64, h, qc0:qc0 + n], True, True,
                               qkeys + [("KP", pslot)], [("ps", sp_[c])])
                        act(EB[es][:, :, 0:n], ps[:, sp_[0]:sp_[0] + 2, 0:n], AF.Exp, [("ps", sp_[0]), ("ps", sp_[1])],
                            [("EB", es)], scale=SCALE)
                        if lk >= 4 * qc:
                            tt("pool", EB[es][:, :, 0:128], EB[es][:, :, 0:128],
                               maskb[:, r, :].unsqueeze(1).broadcast_to([128, 2, 128]), ALU.mult,
                               [("EB", es), "maskb"], [("EB", es)])
                        aeng = "dve"
                        if si == 0:
                            cp(aeng, EACC[:, 1, :], EB[es][:, 1, :], [("EB", es)], ["EACC"])
                        else:
                            tt(aeng, EACC[:, 1, 128 * off:512], EACC[:, 1, 128 * off:512], EB[es][:, 1, 0:n], ALU.add,
                               [("EB", es), "EACC"], ["EACC"])

                    def emit_pv(si):
                        lp, a, lk, r = steps[si]
                        pslot = pslots[lp]
                        off = max(0, lk - 4 * qc)
                        n = 512 - 128 * off
                        g = (step_base[0] + si)
                        es = g % 4
                        first = (si == 0)
                        last = (si == nkb - 1)
                        for c in range(2):
                            mm(ps[:, OB[c], 128 * off:512], VP[pslot][:, r, a * 128:(a + 1) * 128], EB[es][:, c, 0:n],
                               first, last, [("EB", es), ("VP", pslot)], [("ps", OB[c])])
                        mm(ps[:, ZBK, 128 * off:512], onesb[:, :], EB[es][:, 0, 0:n], first, last, [("EB", es), "onesb"], [("ps", ZBK)])

                    step_base = [step]
                    for si in range(nkb + 1):
                        if si < nkb:
                            emit_qk(si)
                        if si >= 1:
                            emit_pv(si - 1)
                        if (step + si) % 9 == 8 and sa_next[0] < N_SLOTS:
                            sa_slot(sa_next[0])
                            sa_next[0] += 1
                    step += nkb
                    qcols = slice(qc * 512, (qc + 1) * 512)
                    act(RZ, ps[:, ZBK, :], AF.Ln, [("ps", ZBK)], ["RZ"])
                    act(RZ, RZ, AF.Exp, ["RZ"], ["RZ"], scale=-1.0)
                    tt("dve", OO[0], ps[:, OB[0], :], RZ, ALU.mult, [("ps", OB[0]), "RZ"], [("OO", 0)])
                    mm(ps[:, ZBK, :], onesf[:, :], EACC[:, 1, :], True, True, ["EACC", "onesf"], [("ps", ZBK)])
                    act(RZ, ps[:, ZBK, :], AF.Ln, [("ps", ZBK)], ["RZ"])
                    act(RZ, RZ, AF.Exp, ["RZ"], ["RZ"], scale=-1.0)
                    tt("dve", RZ, ps[:, OB[1], :], RZ, ALU.mult, [("ps", OB[1]), "RZ"], ["RZ"])
                    stt("dve", OO[0], RZ, nlam, OO[0], ALU.mult, ALU.add, [("OO", 0), "RZ", ("nlam", i)], [("OO", 0)])
                    tt("dve", RZ, OO[0], OO[0], ALU.mult, [("OO", 0)], ["RZ"])
                    mm(ps[:, ZBK, :], onesf[:, :], RZ, True, True, ["RZ", "onesf"], [("ps", ZBK)])
                    act(RZ, ps[:, ZBK, :], AF.Ln, [("ps", ZBK), "eps"], ["RZ"], bias=SM[:, 17:18], scale=1.0)
                    act(RZ, RZ, AF.Exp, ["RZ"], ["RZ"], scale=-0.5)
                    tt("dve", OO[0], OO[0], RZ, ALU.mult, [("OO", 0), "RZ"], [("OO", 0)])
                    act(R1[:, h, qcols], OO[0], AF.Identity, [("OO", 0), ("gsc", i)], [("R1", h, b) for b in range(4 * qc, 4 * qc + 4)], scale=gsc)
            while sa_next[0] < N_SLOTS:
                sa_slot(sa_next[0])
                sa_next[0] += 1

            F0 = FIN[:, 0:256].rearrange("p (n k) -> p n k", k=2)
            P.op("dve", lambda e: e.reciprocal(out=F0, in_=OSsb[:, :, 2:4]), reads=["OSsb"], writes=["PR"])
            tt("dve", F0, OSsb[:, :, 0:2], F0, ALU.mult, ["OSsb", "PR"], ["PR"])
            stt("dve", FIN[:, 256:384], F0[:, :, 1], nlam, F0[:, :, 0], ALU.mult, ALU.add, ["PR", ("nlam", i)], ["PRo"])
            tt("dve", FIN[:, 384:512], FIN[:, 256:384], FIN[:, 256:384], ALU.mult, ["PRo"], ["PRsq"])
            mm(ps[:, SAB, 0:128], onesf[:, :], FIN[:, 384:512], True, True, ["PRsq", "onesf"], [("ps", SAB)])
            act(FIN[:, 384:512], ps[:, SAB, 0:128], AF.Sqrt, [("ps", SAB), "eps"], ["PRr"], bias=SM[:, 17:18], scale=1.0)
            P.op("dve", lambda e: e.reciprocal(out=FIN[:, 384:512], in_=FIN[:, 384:512]), reads=["PRr"], writes=["PRr"])
            tt("dve", FIN[:, 256:384], FIN[:, 256:384], FIN[:, 384:512], ALU.mult, ["PRo", "PRr"], ["PRo"])
            act(OSB, FIN[:, 256:384], AF.Identity, ["PRo", ("gsc", i)], ["OSB"], scale=gsc)
            dma("sp", obd[i], OSB, ["OSB"], [("ob", i)], "ob")
            P.op("pool", lambda e, i=i: e.collective_compute("AllGather", ALU.bypass, replica_groups=[[0, 1, 2, 3], [4, 5, 6, 7]],
                                                              ins=[obd[i]], outs=[og4d[i]]),
                 reads=[("ob", i)], writes=[("og4", i)], dma=("cco4", i), inc=1)
            P.op("pool", lambda e, i=i: e.collective_compute("AllGather", ALU.bypass, replica_groups=[[0, 4], [1, 5], [2, 6], [3, 7]],
                                                              ins=[og4d[i]], outs=[ogd[i]]),
                 reads=[("og4", i)], writes=[("og", i)], dma=("cco", i), inc=1)
            dma("sp", R1[:, :, SAMP0:SAMP0 + 128], ogd[i].rearrange("(h v) n -> v h n", v=128), [("og", i)],
                [("R1", h, 16) for h in range(8)], "ogl")
            P.op("dve", lambda e: e.memset(SM[:, 42:43], 0.0), reads=[], writes=SAKEYS + ["PRo", "xrfence"])
            for b in range(8):
                dma("sp", X[:, b, :], xsp[b * 128:(b + 1) * 128, :], [("xsp", b), "xrfence"], [("X", b)], ("xsp", b % 2))

            fence()
            load_gb(ln1_g, ln1_b, l)
            ots = [w_next() for _ in range(8)]
            oslots = [s for (_, s) in ots]
            for blk in BLOCKS_B:
                c0 = col0_of(blk)
                b0, b1 = proj_tokmajor(blk, lambda k: [("R1", k, blk)], lambda k: R1[:, k, c0:c0 + 128], 8, oslots)
                resid_accum(blk, b0, b1, first=True)
                layer_norm(blk)
                make_XT(blk)
            w_done(ots[-1][0])
            ffn(l, BLOCKS_B, TILES_B, last_layer=(i == n_layers_b - 1))

        for blk in range(16):
            dma("sp", yp[blk * 128:(blk + 1) * 128, :], X[:, blk, :], [("X", blk)], [], ("yo", blk % 4))
        dma("sp", ys, X[:, 16, :], [("X", 16)], [], ("yo", 0))
        P.emit(nc, st)
    return nc, P


_CACHE = {}


def _get_program():
    if "nc" not in _CACHE:
        _CACHE["nc"], _CACHE["P"] = build_program()
    return _CACHE["nc"]


def kernel(x_prompt, x_sample, cache_k, cache_v, state_conv, page_table,
           w_in, conv_w, w_mix_out, w_kv, w_q, w_o,
           lambda_q1, lambda_k1, lambda_q2, lambda_k2, subln_g,
           ln1_g, ln1_b, w_gate, w_up, w_down, ln2_g, ln2_b):
    f32 = np.float32
    A = lambda a: np.ascontiguousarray(np.asarray(a), dtype=f32)
    x_prompt = A(x_prompt); x_sample = A(x_sample); state_conv = A(state_conv)
    cache_k = np.asarray(cache_k); cache_v = np.asarray(cache_v)
    page_table = np.asarray(page_table).astype(np.int32)
    w_in = A(w_in); conv_w = A(conv_w); w_mix_out = A(w_mix_out); w_kv = A(w_kv); w_q = A(w_q); w_o = A(w_o)
    w_gate = A(w_gate); w_up = A(w_up); w_down = A(w_down)
    ln1_g = A(ln1_g); ln1_b = A(ln1_b); ln2_g = A(ln2_g); ln2_b = A(ln2_b); subln_g = A(subln_g)
    lam = np.concatenate([A(lambda_q1).reshape(-1), A(lambda_k1).reshape(-1),
                          A(lambda_q2).reshape(-1), A(lambda_k2).reshape(-1)]).reshape(1, 512)
    cw = np.ascontiguousarray(conv_w.reshape(2, 3, 8, 128).transpose(3, 0, 1, 2).reshape(128, 48))
    ptrep = np.ascontiguousarray(np.repeat(page_table.T, 8, axis=0))
    pmod = (np.arange(128) % 8).astype(f32).reshape(128, 1)
    xs = x_sample.reshape(128, D)
    p_idx = np.arange(128)
    in_maps = []
    for r in range(8):
        s, j = r // 4, r % 4
        xp = np.empty((2048, D), f32)
        xh = np.zeros((64, D), f32)
        hv = np.zeros((128, 64), f32)
        pos = np.zeros((128, 18), f32)
        for l in range(NB):
            t0 = (4 * l + j) * 128
            xp[l * 128:(l + 1) * 128] = x_prompt[s, t0:t0 + 128]
            if t0 >= 4:
                xh[l * 4:(l + 1) * 4] = x_prompt[s, t0 - 4:t0]
                hv[:, l * 4:(l + 1) * 4] = 1.0
            pos[:, l] = t0 + p_idx
        pos[:, 16] = 2048.0
        masks = np.zeros((128, 4, 128), f32)
        for rr in range(4):
            masks[:, rr, :] = ((rr * 128 + p_idx)[:, None] <= (j * 128 + p_idx)[None, :]).astype(f32)
        wq_h = np.ascontiguousarray(w_q[:, :, r * 128:(r + 1) * 128])
        wkv_h = np.ascontiguousarray(np.concatenate([w_kv[:, r * 128:(r + 1) * 128], w_kv[:, D + r * 128:D + (r + 1) * 128]], axis=1))
        ck = np.ascontiguousarray(cache_k[:, :, r], dtype=f32).reshape(N_POOL * 8, 2048)
        cv = np.ascontiguousarray(cache_v[:, :, r], dtype=f32).reshape(N_POOL * 8, 2048)
        in_maps.append(dict(
            xp=xp, xs=xs, xh=xh, hv=hv, pos=pos, masks=masks.reshape(128, 512), pmod=pmod,
            w_in=w_in, cw=cw, w_mix_out=w_mix_out, w_kv=w_kv, w_q=w_q, w_o=w_o,
            w_gate=w_gate, w_up=w_up, w_down=w_down,
            ln1_g=ln1_g, ln1_b=ln1_b, ln2_g=ln2_g, ln2_b=ln2_b, subln_g=subln_g, lam=lam,
            wq_h=wq_h, wkv_h=wkv_h, ck=ck, cv=cv, ptrep=ptrep, state_conv=state_conv))
    nc = _get_program()
    res = run_bass_kernel_spmd(nc, in_maps, core_ids=list(range(8))).results
    y_prompt = np.empty((2, 8192, D), f32)
    k_prompt = np.empty((2, 8192, D), f32)
    v_prompt = np.empty((2, 8192, D), f32)
    conv_prompt = np.empty((2, 2, 2, D), f32)
    for r in range(8):
        s, j = r // 4, r % 4
        for l in range(NB):
            t0 = (4 * l + j) * 128
            y_prompt[s, t0:t0 + 128] = res[r]["yp"][l * 128:(l + 1) * 128]
            k_prompt[s, t0:t0 + 128] = res[r]["kp"][l * 128:(l + 1) * 128]
            v_prompt[s, t0:t0 + 128] = res[r]["vp"][l * 128:(l + 1) * 128]
        if j == 3:
            conv_prompt[:, s] = res[r]["cpo"]
    y_sample = np.asarray(res[0]["ys"], f32).reshape(128, 1, D)
    k_sample = np.asarray(res[0]["ks"], f32).reshape(128, 1, 8, 2, 64)
    v_sample = np.asarray(res[0]["vs"], f32).reshape(128, 1, 8, 128)
    conv_sample = np.asarray(res[0]["cso"], f32).reshape(2, 128, 2, D)
    return (y_prompt, y_sample, k_prompt.reshape(2, 8192, 8, 2, 64), v_prompt.reshape(2, 8192, 8, 128),
            conv_prompt, k_sample, v_sample, conv_sample)
```

```python
import math
import numpy as np
from contextlib import ExitStack
import concourse.bass as bass
import concourse.mybir as mybir
from concourse.bass_utils import run_bass_kernel_spmd

F32 = mybir.dt.float32
BF16 = mybir.dt.bfloat16
I32 = mybir.dt.int32
ALU = mybir.AluOpType
AF = mybir.ActivationFunctionType
AX = mybir.AxisListType

D = 1024
KC = 8
DFF = 2816
NFF = 22
NB = 16
NT = 2240
SAMP0 = 2048
HALO0 = 2176
NH = 8
ALPHA = 8.0 ** 0.25
EPS = 1e-5
ROPE_THETA = 500000.0
NW = 10
N_POOL = 2560
SCALE = 0.125
FF_GROUPS = [list(range(0, 8)), list(range(8, 16)), list(range(16, 22))]


class Prog:
    ENGS = ("pe", "act", "dve", "pool", "sp")

    def __init__(self):
        self.ops = []
        self.last_writer = {}
        self.readers = {}
        self.dma_sem_counts = {}

    def op(self, eng, fn, reads=(), writes=(), dma=None, inc=16):
        idx = len(self.ops)
        deps = set()
        for k in reads:
            w = self.last_writer.get(k)
            if w is not None:
                deps.add(w)
        for k in writes:
            w = self.last_writer.get(k)
            if w is not None:
                deps.add(w)
            for r in self.readers.get(k, ()):
                deps.add(r)
        for k in reads:
            self.readers.setdefault(k, []).append(idx)
        for k in writes:
            self.last_writer[k] = idx
            self.readers[k] = []
        deps.discard(idx)
        rec = dict(eng=eng, fn=fn, deps=deps, dma=dma, inc=inc, has_cons=False, count=None)
        if dma is not None:
            c = self.dma_sem_counts.get(dma, 0) + inc
            self.dma_sem_counts[dma] = c
            rec["count"] = c
        self.ops.append(rec)
        return idx

    def emit(self, nc, stack):
        ops = self.ops
        for o in ops:
            for d in o["deps"]:
                p = ops[d]
                if p["eng"] == "pe" and o["eng"] == "pe" and p["dma"] is None:
                    continue
                p["has_cons"] = True
        esem = {e: stack.enter_context(nc.semaphore("es_" + e)) for e in self.ENGS}
        dsem = {k: stack.enter_context(nc.semaphore("ds_%d" % i)) for i, k in enumerate(self.dma_sem_counts)}
        cnt = {e: 0 for e in self.ENGS}
        for o in ops:
            if o["dma"] is None and o["has_cons"]:
                cnt[o["eng"]] += 1
                o["count"] = cnt[o["eng"]]
        per_eng = {e: [] for e in self.ENGS}
        running = {}
        for o in ops:
            waits = {}
            for d in o["deps"]:
                p = ops[d]
                if p["dma"] is not None:
                    key = ("d", p["dma"])
                    val = running[p["dma"]]
                else:
                    if p["eng"] == "pe" and o["eng"] == "pe":
                        continue
                    key = ("e", p["eng"])
                    val = p["count"]
                if val > waits.get(key, 0):
                    waits[key] = val
            if o["dma"] is not None:
                running[o["dma"]] = o["count"]
            per_eng[o["eng"]].append((o, waits))
        final_waits = {}
        for o in ops:
            if o["dma"] is not None:
                final_waits.setdefault(o["eng"], {})[o["dma"]] = self.dma_sem_counts[o["dma"]]
        block = stack.enter_context(nc.Block())

        def make(engname):
            def body(eng):
                known = {}
                for o, waits in per_eng[engname]:
                    for key, val in waits.items():
                        if known.get(key, 0) >= val:
                            continue
                        sem = dsem[key[1]] if key[0] == "d" else esem[key[1]]
                        eng.wait_ge(sem, val)
                        known[key] = val
                    ins = o["fn"](eng)
                    if o["dma"] is not None:
                        if o["inc"] == 16:
                            ins.then_inc(dsem[o["dma"]], 16)
                        else:
                            ins.then_inc(dsem[o["dma"]])
                    elif o["has_cons"]:
                        ins.then_inc(esem[engname], 1)
                for k, c in final_waits.get(engname, {}).items():
                    eng.wait_ge(dsem[k], c)
            return body

        block.tensor(make("pe"))
        block.scalar(make("act"))
        block.vector(make("dve"))
        block.gpsimd(make("pool"))
        block.sync(make("sp"))


def blocks_of(c0, n):
    return list(range(c0 // 128, (c0 + n - 1) // 128 + 1))


def build_program(n_layers_b=2):
    nc = bass.Bass("TRN2", target_bir_lowering=False)
    P = Prog()

    def din(name, shape, dt=F32):
        return nc.dram_tensor(name, shape, dt, kind="ExternalInput").ap()

    def dout(name, shape, dt=F32):
        return nc.dram_tensor(name, shape, dt, kind="ExternalOutput").ap()

    xp = din("xp", [2048, D]); xs = din("xs", [128, D]); xh = din("xh", [64, D])
    hv = din("hv", [128, 64]); posd = din("pos", [128, 18]); maskd = din("masks", [128, 512])
    pmod = din("pmod", [128, 1])
    w_in = din("w_in", [2, D, 3 * D]); cwd = din("cw", [128, 48]); w_mo = din("w_mix_out", [2, D, D])
    w_kv = din("w_kv", [D, 2 * D]); w_q = din("w_q", [2, D, D]); w_o = din("w_o", [2, D, D])
    w_gate = din("w_gate", [4, D, DFF]); w_up = din("w_up", [4, D, DFF]); w_down = din("w_down", [4, DFF, D])
    ln1_g = din("ln1_g", [4, D]); ln1_b = din("ln1_b", [4, D]); ln2_g = din("ln2_g", [4, D]); ln2_b = din("ln2_b", [4, D])
    subg = din("subln_g", [2, 128]); lamd = din("lam", [1, 512])
    wq_h = din("wq_h", [2, D, 128]); wkv_h = din("wkv_h", [D, 256])
    ckd = din("ck", [N_POOL * 8, 2048]); cvd = din("cv", [N_POOL * 8, 2048])
    ptrep = din("ptrep", [128, 128], I32)
    stc = din("state_conv", [2, 128, 2, D])

    yp = dout("yp", [2048, D]); ys = dout("ys", [128, D])
    kp = dout("kp", [2048, D]); vp = dout("vp", [2048, D])
    cpo = dout("cpo", [2, 2, D]); ks = dout("ks", [128, D]); vs = dout("vs", [128, D])
    cso = dout("cso", [2, 128, 2, D])

    kvb = [nc.dram_tensor("kvb%d" % h, [256, 2048], BF16).ap() for h in range(NH)]
    kvg = [nc.dram_tensor("kvg%d" % h, [1024, 2048], BF16).ap() for h in range(NH)]
    obd = [nc.dram_tensor("ob%d" % i, [128, 128], BF16).ap() for i in range(2)]
    ogd = [nc.dram_tensor("og%d" % i, [1024, 128], BF16).ap() for i in range(2)]
    og4d = [nc.dram_tensor("og4_%d" % i, [512, 128], BF16).ap() for i in range(2)]
    qhd = [nc.dram_tensor("qhd%d" % i, [128, 128], F32).ap() for i in range(2)]
    xsp = nc.dram_tensor("xsp", [1024, D], F32).ap()

    st = ExitStack()
    with st:
        def sb(name, shape, dt):
            return st.enter_context(nc.sbuf_tensor(name, shape, dt))

        X = sb("X", [128, 18, D], F32)
        XT = sb("XT", [128, 8, NT], BF16)
        R1 = sb("R1", [128, 8, NT], BF16)
        R2 = sb("R2", [128, 8352], F32)
        Wt = sb("Wt", [128, NW, 1024], BF16)
        identf = sb("identf", [128, 128], F32)
        onesf = sb("onesf", [128, 128], F32)
        onesb = sb("onesb", [128, 128], BF16)
        cw = sb("cwt", [128, 48], F32)
        hvt = sb("hvt", [128, 64], F32)
        post = sb("post", [128, 18], F32)
        maskb = sb("maskb", [128, 4, 128], BF16)
        CS = sb("CS", [128, 18, 16], F32)
        inv8 = sb("inv8", [128, 8], F32)
        SM = sb("SM", [128, 64], F32)
        lamt = sb("lamt", [128, 512], F32)
        gcol = sb("gcol", [128, 2], F32)
        pmt = sb("pmt", [128, 1], F32)
        ptt = sb("ptt", [128, 128], I32)
        IDX = sb("IDX", [128, 128], I32)
        QH = sb("QH", [128, 128], F32)
        KH = sb("KH", [128, 128], F32)
        VH = sb("VH", [128, 128], F32)
        VHb = sb("VHb", [128, 128], BF16)
        CPS = sb("CPS", [128, 2, 128], F32)
        identb = sb("identb", [128, 128], BF16)
        maskneg = sb("maskneg", [128, 4, 128], BF16)
        ps = st.enter_context(nc.psum_tensor("ps", [128, 8, 512], F32))

        TT = R2[:, 0:288].rearrange("p (a b) -> p a b", a=18)
        TF = R2[:, 288:576].rearrange("p (a b) -> p a b", a=18)
        TI = R2[:, 576:864].bitcast(I32).rearrange("p (a b) -> p a b", a=18)
        R1f = R1[:, :, :].rearrange("p a b -> p (a b)")
        R2b = R2[:, :].bitcast(BF16)

        bank_ctr = [0]

        def nb():
            b = bank_ctr[0] % 8
            bank_ctr[0] += 1
            return b

        def dma(eng, out, in_, reads, writes, key):
            return P.op(eng, lambda e: e.dma_start(out=out, in_=in_), reads=reads, writes=writes, dma=key)

        def mm(out, lhsT, rhs, start, stop, reads, writes):
            return P.op("pe", lambda e: e.matmul(out, lhsT=lhsT, rhs=rhs, start=start, stop=stop), reads=reads, writes=writes)

        def tr(out, in_, rows, reads, writes):
            return P.op("pe", lambda e: e.transpose(out=out, in_=in_, identity=identf[0:rows, 0:rows]), reads=list(reads) + ["identf"], writes=writes)

        def act(out, in_, func, reads, writes, bias=None, scale=None, accum_out=None):
            kw = {}
            if bias is not None:
                kw["bias"] = bias
            if scale is not None:
                kw["scale"] = scale
            if accum_out is not None:
                kw["accum_out"] = accum_out
            return P.op("act", lambda e: e.activation(out=out, in_=in_, func=func, **kw), reads=reads, writes=writes)

        def tt(eng, out, in0, in1, op, reads, writes):
            return P.op(eng, lambda e: e.tensor_tensor(out=out, in0=in0, in1=in1, op=op), reads=reads, writes=writes)

        def ts(eng, out, in0, s1, op0, reads, writes, s2=None, op1=None):
            if op1 is None:
                return P.op(eng, lambda e: e.tensor_scalar(out=out, in0=in0, scalar1=s1, scalar2=None, op0=op0), reads=reads, writes=writes)
            return P.op(eng, lambda e: e.tensor_scalar(out=out, in0=in0, scalar1=s1, scalar2=s2, op0=op0, op1=op1), reads=reads, writes=writes)

        def stt(eng, out, in0, scalar, in1, op0, op1, reads, writes):
            return P.op(eng, lambda e: e.scalar_tensor_tensor(out=out, in0=in0, scalar=scalar, in1=in1, op0=op0, op1=op1), reads=reads, writes=writes)

        def cp(eng, out, in_, reads, writes):
            return P.op(eng, lambda e: e.tensor_copy(out=out, in_=in_), reads=reads, writes=writes)

        R2KEYS = ["Uext", "Umisc", "T0", ("hsb", 0), ("hsb", 1), ("STin", 0), ("STin", 1), ("stg", 0), ("stg", 1), "GB",
                  ("ssb", 0), ("ssb", 1), ("Ksb", 0), ("Ksb", 1), "rt0", "rt1", "rt2", "rt3", "rt0b", ("KP", 0), ("KP", 1),
                  ("VP", 0), ("VP", 1), ("EB", 0), ("EB", 1), ("EB", 2), ("EB", 3), "RZ", "EACC", ("OO", 0), "TT", "TI", "TF", "IDXF"]

        def fence():
            P.op("dve", lambda e: e.memset(SM[:, 40:41], 0.0), reads=[], writes=list(R2KEYS))

        wplan = []
        wstate = dict(next_load=0, next_use=0)

        def w_request(kind, src):
            wplan.append((kind, src))
            return len(wplan) - 1

        def w_emit_load(t):
            kind, src = wplan[t]
            s = t % NW
            if kind == "A":
                dst = Wt[:, s, :].rearrange("p (k o) -> p k o", k=8)
            else:
                dst = Wt[:, s, :]
            dma("pool", dst, src, reads=[], writes=[("W", s)], key=("w", s))

        def w_slot(t):
            while wstate["next_load"] <= t:
                w_emit_load(wstate["next_load"])
                wstate["next_load"] += 1
            return t % NW

        def w_done(t):
            lim = min(t + NW, len(wplan) - 1)
            while wstate["next_load"] <= lim:
                w_emit_load(wstate["next_load"])
                wstate["next_load"] += 1

        def plan_all():
            for l in range(2):
                for f in range(8):
                    for part in (1, 2, 0):
                        c0 = part * D + f * 128
                        w_request("A", w_in[l].rearrange("(k p) o -> p k o", p=128)[:, :, c0:c0 + 128])
                for f in range(8):
                    w_request("B", w_mo[l][f * 128:(f + 1) * 128, :])
                plan_ffn(l)
            for half in range(2):
                for kc in range(8):
                    w_request("B", w_kv[kc * 128:(kc + 1) * 128, half * D:(half + 1) * D])
            for i in range(n_layers_b):
                w_request("A", wq_h[i].rearrange("(k p) o -> p k o", p=128))
                if i == 0:
                    w_request("A", wkv_h.rearrange("(k p) o -> p k o", p=128)[:, :, 0:128])
                    w_request("A", wkv_h.rearrange("(k p) o -> p k o", p=128)[:, :, 128:256])
                for kc in range(8):
                    w_request("B", w_q[i][kc * 128:(kc + 1) * 128, :])
                for h in range(8):
                    w_request("B", w_o[i][h * 128:(h + 1) * 128, :])
                plan_ffn(2 + i)

        def plan_ffn(l):
            for grp in FF_GROUPS:
                for c in grp:
                    w_request("A", w_gate[l].rearrange("(k p) o -> p k o", p=128)[:, :, c * 128:(c + 1) * 128])
                    w_request("A", w_up[l].rearrange("(k p) o -> p k o", p=128)[:, :, c * 128:(c + 1) * 128])
                for c in grp:
                    w_request("B", w_down[l][c * 128:(c + 1) * 128, :])

        plan_all()
        wcur = [0]

        def w_next():
            t = wcur[0]
            wcur[0] += 1
            return t, w_slot(t)

        P.op("pool", lambda e: e.memset(identf[:, :], 0.0), writes=["identf"])
        P.op("pool", lambda e: e.affine_select(out=identf[:, :], in_=identf[:, :], compare_op=ALU.not_equal, fill=1.0,
                                               base=0, pattern=[[-1, 128]], channel_multiplier=1), reads=["identf"], writes=["identf"])
        P.op("pool", lambda e: e.memset(onesf[:, :], 1.0), writes=["onesf"])
        P.op("pool", lambda e: e.memset(onesb[:, :], 1.0), writes=["onesb"])
        for i in range(8):
            v = float(ROPE_THETA ** (-(2.0 * i) / 16.0))
            P.op("pool", lambda e, i=i, v=v: e.memset(inv8[:, i:i + 1], v), writes=["inv8"])
        P.op("pool", lambda e: e.memset(SM[:, 16:17], EPS), writes=["eps"])
        P.op("pool", lambda e: e.memset(SM[:, 17:18], 128.0 * EPS), writes=["eps"])
        dma("sp", cw[:, :], cwd, [], ["cw"], "c0")
        dma("sp", hvt[:, :], hv, [], ["hvt"], "c0")
        dma("sp", post[:, :], posd, [], ["post"], "c0")
        dma("sp", pmt[:, :], pmod, [], ["pmt"], "c0")
        dma("sp", ptt[:, :], ptrep, [], ["ptt"], "c0")
        dma("sp", lamt[:, :], lamd.partition_broadcast(128), [], ["lamt"], "c0")
        dma("pool", maskb[:, :, :], maskd.rearrange("p (r q) -> p r q", r=4), [], ["maskb"], "c1")
        for i in range(2):
            dma("sp", gcol[:, i:i + 1], subg[i:i + 1, :].rearrange("o v -> v o"), [], ["gcol"], "c0")
        for l in range(NB):
            dma("sp", X[:, l, :], xp[l * 128:(l + 1) * 128, :], [], [("X", l)], ("x", l % 4))
        dma("sp", X[:, 16, :], xs, [], [("X", 16)], ("x", 0))
        dma("sp", X[0:64, 17, :], xh, [], [("X", 17)], ("x", 1))

        cp("dve", identb[:, :], identf[:, :], ["identf"], ["identb"])
        ts("dve", maskneg[:, :, :], maskb[:, :, :], -1.0, ALU.add, ["maskb"], ["maskneg"], s2=30000.0, op1=ALU.mult)
        tt("dve", TT[:, :, 8:16], post[:, :].unsqueeze(2).broadcast_to([128, 18, 8]),
           inv8[:, :].unsqueeze(1).broadcast_to([128, 18, 8]), ALU.mult, ["post", "inv8"], ["TT"])
        ts("dve", TT[:, :, 8:16], TT[:, :, 8:16], float(1.0 / (2 * math.pi)), ALU.mult, ["TT"], ["TT"])
        ts("dve", TT[:, :, 0:8], TT[:, :, 8:16], 0.25, ALU.add, ["TT"], ["TT"])
        cp("dve", TI[:, :, :], TT[:, :, :], ["TT"], ["TI"])
        cp("dve", TF[:, :, :], TI[:, :, :], ["TI"], ["TF"])
        tt("dve", TT[:, :, :], TT[:, :, :], TF[:, :, :], ALU.subtract, ["TT", "TF"], ["TT"])
        stt("dve", TF[:, :, :], TT[:, :, :], 0.5, TT[:, :, :], ALU.is_gt, ALU.subtract, ["TT"], ["TF"])
        stt("dve", TT[:, :, :], TF[:, :, :], 0.5, TF[:, :, :], ALU.is_gt, ALU.subtract, ["TF"], ["TT"])
        act(CS[:, :, :], TT[:, :, :], AF.Sin, ["TT"], ["CS"], scale=float(2 * math.pi))

        IDXF = R2[:, 1024:1152]
        ts("dve", IDXF, ptt[:, :], 8.0, ALU.mult, ["ptt"], ["IDXF"], s2=pmt[:, 0:1], op1=ALU.add)
        cp("dve", IDX[:, :], IDXF, ["IDXF"], ["IDX"])

        def rows_of(blk):
            return 64 if blk == 17 else 128

        def col0_of(blk):
            return blk * 128 if blk < 17 else HALO0

        def make_XT(blk):
            rows = rows_of(blk)
            c0 = col0_of(blk)
            for half in range(2):
                b = nb()
                for k in range(4):
                    kk = half * 4 + k
                    tr(ps[:, b, k * 128:k * 128 + rows], X[0:rows, blk, kk * 128:(kk + 1) * 128], rows,
                       [("X", blk)], [("ps", b)])
                src = ps[:, b, :].rearrange("p (k n) -> p k n", k=4)[:, :, 0:rows]
                dst = XT[:, half * 4:(half + 1) * 4, c0:c0 + rows]
                if (blk + half) % 2 == 0:
                    act(dst, src, AF.Copy, [("ps", b)], [("XT", blk)])
                else:
                    cp("dve", dst, src, [("ps", b)], [("XT", blk)])

        GBv = R2[:, 6304:8352]

        def load_gb(gd, bd, l):
            dma("sp", GBv[:, 0:1024], gd[l:l + 1, :].partition_broadcast(128), [], ["GB"], "gb")
            dma("sp", GBv[:, 1024:2048], bd[l:l + 1, :].partition_broadcast(128), [], ["GB"], "gb")

        def layer_norm(blk):
            rows = rows_of(blk)
            xk = ("X", blk)
            xb = X[0:rows, blk, :]
            P.op("dve", lambda e: e.bn_stats(out=SM[0:rows, 0:6], in_=X[0:rows, blk, 0:512]), reads=[xk], writes=["st0"])
            P.op("dve", lambda e: e.bn_stats(out=SM[0:rows, 6:12], in_=X[0:rows, blk, 512:1024]), reads=[xk], writes=["st1"])
            P.op("dve", lambda e: e.bn_aggr(out=SM[0:rows, 12:14], in_=SM[0:rows, 0:12]), reads=["st0", "st1"], writes=["mv"])
            act(SM[0:rows, 14:15], SM[0:rows, 13:14], AF.Ln, ["mv", "eps"], ["rstd"], bias=SM[0:rows, 16:17], scale=1.0)
            act(SM[0:rows, 14:15], SM[0:rows, 14:15], AF.Exp, ["rstd"], ["rstd"], scale=-0.5)
            stt("dve", SM[0:rows, 15:16], SM[0:rows, 12:13], -1.0, SM[0:rows, 14:15], ALU.mult, ALU.mult, ["mv", "rstd"], ["nmr"])
            act(xb, xb, AF.Identity, [xk, "rstd", "nmr"], [xk], bias=SM[0:rows, 15:16], scale=SM[0:rows, 14:15])
            tt("dve", xb, xb, GBv[0:rows, 0:1024], ALU.mult, [xk, "GB"], [xk])
            tt("dve", xb, xb, GBv[0:rows, 1024:2048], ALU.add, [xk, "GB"], [xk])

        def resid_accum(blk, b0, b1, first):
            rows = rows_of(blk)
            xk = ("X", blk)
            for half, b in ((0, b0), (1, b1)):
                xs_ = X[0:rows, blk, half * 512:(half + 1) * 512]
                if first:
                    stt("dve", xs_, xs_, ALPHA, ps[0:rows, b, :], ALU.mult, ALU.add, [xk, ("ps", b)], [xk])
                else:
                    tt("dve", xs_, xs_, ps[0:rows, b, :], ALU.add, [xk, ("ps", b)], [xk])

        def proj_tokmajor(blk, src_keys_fn, lhs_fn, nk, wslots, first=True):
            rows = rows_of(blk)
            b0, b1 = nb(), nb()
            for half, b in ((0, b0), (1, b1)):
                for k in range(nk):
                    mm(ps[0:rows, b, :], lhs_fn(k), Wt[:, wslots[k], half * 512:(half + 1) * 512], k == 0, k == nk - 1,
                       list(src_keys_fn(k)) + [("W", wslots[k])], [("ps", b)])
            return b0, b1

        def ffn(l, blocks, tiles, last_layer):
            fence()
            load_gb(ln2_g, ln2_b, l)
            for gi, grp in enumerate(FF_GROUPS):
                for ci, c in enumerate(grp):
                    tg, sg = w_next()
                    tu, su = w_next()
                    for (c0, n) in tiles:
                        bg, bu = nb(), nb()
                        xkeys = [("XT", b) for b in blocks_of(c0, n)]
                        for k in range(8):
                            mm(ps[:, bg, 0:n], Wt[:, sg, k * 128:(k + 1) * 128], XT[:, k, c0:c0 + n], k == 0, k == 7,
                               xkeys + [("W", sg)], [("ps", bg)])
                        for k in range(8):
                            mm(ps[:, bu, 0:n], Wt[:, su, k * 128:(k + 1) * 128], XT[:, k, c0:c0 + n], k == 0, k == 7,
                               xkeys + [("W", su)], [("ps", bu)])
                        slot = bank_ctr[0] % 2
                        ssb = R2[:, slot * 512:slot * 512 + n]
                        act(ssb, ps[:, bg, 0:n], AF.Silu, [("ps", bg)], [("ssb", slot)])
                        tt("dve", R1[:, ci, c0:c0 + n], ps[:, bu, 0:n], ssb, ALU.mult, [("ps", bu), ("ssb", slot)],
                           [("R1", ci, b) for b in blocks_of(c0, n)])
                    w_done(tu)
                dts = [w_next() for _ in grp]
                dslots = [s for (_, s) in dts]
                for blk in blocks:
                    c0 = col0_of(blk)
                    rows = rows_of(blk)
                    b0, b1 = proj_tokmajor(blk, lambda k: [("R1", k, blk)], lambda k: R1[:, k, c0:c0 + rows], len(grp), dslots)
                    resid_accum(blk, b0, b1, first=(gi == 0))
                    if gi == len(FF_GROUPS) - 1:
                        layer_norm(blk)
                        if not last_layer:
                            make_XT(blk)
                w_done(dts[-1][0])

        for blk in range(18):
            make_XT(blk)

        TILES_A = [(2048, 192), (0, 512), (512, 512), (1024, 512), (1536, 512)]
        TILES_B = [(2048, 128), (0, 512), (512, 512), (1024, 512), (1536, 512)]
        BLOCKS_A = list(range(18))
        BLOCKS_B = list(range(17))

        Uext = R2[:, 0:2080].rearrange("p (l t) -> p l t", l=16)
        Umisc = R2[:, 2080:2272]
        T0 = R2[:, 2272:4512]
        T0p = T0[:, 0:2048].rearrange("p (l t) -> p l t", l=16)
        T0h = T0[:, HALO0:NT].rearrange("p (l t) -> p l t", l=16)
        Uh = Umisc[:, 128:192].rearrange("p (l t) -> p l t", l=16)

        for l in range(2):
            fence()
            load_gb(ln1_g, ln1_b, l)
            dma("sp", cso[l, :, 0, :], stc[l, :, 1, :], [], [], "misc")
            for f in range(8):
                tc_, sc_ = w_next()
                th_, sh_ = w_next()
                tb_, sb_ = w_next()
                sts = f % 2
                STin = R2[:, 5536 + sts * 256:5536 + (sts + 1) * 256].rearrange("p (a b) -> p a b", a=2)
                dma("sp", STin, stc[l, :, :, f * 128:(f + 1) * 128], [], [("STin", sts)], ("st", sts))
                for (c0, n) in TILES_A:
                    bc, bh = nb(), nb()
                    xkeys = [("XT", b) for b in blocks_of(c0, n)]
                    for k in range(8):
                        mm(ps[:, bc, 0:n], Wt[:, sc_, k * 128:(k + 1) * 128], XT[:, k, c0:c0 + n], k == 0, k == 7,
                           xkeys + [("W", sc_)], [("ps", bc)])
                    for k in range(8):
                        mm(ps[:, bh, 0:n], Wt[:, sh_, k * 128:(k + 1) * 128], XT[:, k, c0:c0 + n], k == 0, k == 7,
                           xkeys + [("W", sh_)], [("ps", bh)])
                    hs = bank_ctr[0] % 2
                    hsb = R2[:, 4512 + hs * 512:4512 + hs * 512 + n]
                    act(hsb, ps[:, bh, 0:n], AF.Copy, [("ps", bh)], [("hsb", hs)])
                    if c0 == SAMP0:
                        tt("dve", Umisc[:, 0:n], ps[:, bc, 0:n], hsb, ALU.mult, [("ps", bc), ("hsb", hs)], ["Umisc"])
                        tt("dve", Umisc[:, 128:192], Umisc[:, 128:192], hvt[:, :], ALU.mult, ["Umisc", "hvt"], ["Umisc"])
                        cp("dve", Uext[:, :, 0:2], Uh[:, :, 2:4], ["Umisc"], ["Uext"])
                    else:
                        l0 = c0 // 128
                        tt("dve", Uext[:, l0:l0 + 4, 2:130], ps[:, bc, 0:n].rearrange("p (l t) -> p l t", l=4),
                           hsb.rearrange("p (l t) -> p l t", l=4), ALU.mult, [("ps", bc), ("hsb", hs)], ["Uext"])
                w_done(th_)
                wc = lambda tap: cw[:, (l * 3 + tap) * 8 + f:(l * 3 + tap) * 8 + f + 1]
                ts("dve", T0p, Uext[:, :, 2:130], wc(2), ALU.mult, ["Uext", "cw"], ["T0"])
                stt("dve", T0p, Uext[:, :, 1:129], wc(1), T0p, ALU.mult, ALU.add, ["Uext", "T0", "cw"], ["T0"])
                stt("dve", T0p, Uext[:, :, 0:128], wc(0), T0p, ALU.mult, ALU.add, ["Uext", "T0", "cw"], ["T0"])
                bs = nb()
                for a in range(2):
                    tr(ps[:, bs, a * 128:(a + 1) * 128], STin[:, a, :], 128, [("STin", sts)], [("ps", bs)])
                T0s = T0[:, SAMP0:SAMP0 + 128]
                ts("dve", T0s, Umisc[:, 0:128], wc(2), ALU.mult, ["Umisc", "cw"], ["T0"])
                stt("dve", T0s, ps[:, bs, 128:256], wc(1), T0s, ALU.mult, ALU.add, [("ps", bs), "T0", "cw"], ["T0"])
                stt("dve", T0s, ps[:, bs, 0:128], wc(0), T0s, ALU.mult, ALU.add, [("ps", bs), "T0", "cw"], ["T0"])
                ts("dve", T0h, Uh, wc(2), ALU.mult, ["Umisc", "cw"], ["T0"])
                stt("dve", T0h[:, :, 2:4], Uh[:, :, 1:3], wc(1), T0h[:, :, 2:4], ALU.mult, ALU.add, ["Umisc", "T0", "cw"], ["T0"])
                stt("dve", T0h[:, :, 2:4], Uh[:, :, 0:2], wc(0), T0h[:, :, 2:4], ALU.mult, ALU.add, ["Umisc", "T0", "cw"], ["T0"])
                bo = nb()
                tr(ps[:, bo, 0:128], Umisc[:, 0:128], 128, ["Umisc"], [("ps", bo)])
                tr(ps[0:2, bo, 128:256], Uext[:, 15, 128:130], 128, ["Uext"], [("ps", bo)])
                sg = f % 2
                stg = R2[:, 6048 + sg * 128:6048 + (sg + 1) * 128]
                act(stg, ps[:, bo, 0:128], AF.Copy, [("ps", bo)], [("stg", sg)])
                dma("sp", cso[l, :, 1, f * 128:(f + 1) * 128], stg, [("stg", sg)], [], ("so", sg))
                act(CPS[0:2, sg, :], ps[0:2, bo, 128:256], AF.Copy, [("ps", bo)], [("cps", sg)])
                dma("sp", cpo[l, :, f * 128:(f + 1) * 128], CPS[0:2, sg, :], [("cps", sg)], [], ("so", sg))
                for (c0, n) in TILES_A:
                    bb = nb()
                    xkeys = [("XT", b) for b in blocks_of(c0, n)]
                    for k in range(8):
                        mm(ps[:, bb, 0:n], Wt[:, sb_, k * 128:(k + 1) * 128], XT[:, k, c0:c0 + n], k == 0, k == 7,
                           xkeys + [("W", sb_)], [("ps", bb)])
                    tt("dve", R1[:, f, c0:c0 + n], ps[:, bb, 0:n], T0[:, c0:c0 + n], ALU.mult, [("ps", bb), "T0"],
                       [("R1", f, b) for b in blocks_of(c0, n)])
                w_done(tb_)
            ots = [w_next() for _ in range(8)]
            oslots = [s for (_, s) in ots]
            for blk in BLOCKS_A:
                c0 = col0_of(blk)
                rows = rows_of(blk)
                b0, b1 = proj_tokmajor(blk, lambda k: [("R1", k, blk)], lambda k: R1[:, k, c0:c0 + rows], 8, oslots)
                resid_accum(blk, b0, b1, first=True)
                layer_norm(blk)
                make_XT(blk)
            w_done(ots[-1][0])
            ffn(l, BLOCKS_A, TILES_A, last_layer=False)

        Ksb = [R2[:, 0:1024], R2[:, 1024:2048]]
        RT = R2[:, 2048:2560]

        def rope(eng, T, groups, blk, rows, key):
            Tv = T.rearrange("p (g d) -> p g d", g=groups)
            x1 = Tv[:, :, 0:8]
            x2 = Tv[:, :, 8:16]
            cosb = CS[0:rows, blk, 0:8].unsqueeze(1).broadcast_to([rows, groups, 8])
            sinb = CS[0:rows, blk, 8:16].unsqueeze(1).broadcast_to([rows, groups, 8])
            t = [RT[0:rows, i * 128:i * 128 + groups * 8].rearrange("p (g d) -> p g d", g=groups) for i in range(4)]
            tt(eng, t[0], x1, cosb, ALU.mult, [key, "CS"], ["rt0"])
            tt(eng, t[1], x2, sinb, ALU.mult, [key, "CS"], ["rt1"])
            tt(eng, t[2], x2, cosb, ALU.mult, [key, "CS"], ["rt2"])
            tt(eng, t[3], x1, sinb, ALU.mult, [key, "CS"], ["rt3"])
            tt(eng, x1, t[0], t[1], ALU.subtract, ["rt0", "rt1", key], [key])
            tt(eng, x2, t[2], t[3], ALU.add, ["rt2", "rt3", key], [key])

        R1ALL = [("R1", a, b) for a in range(8) for b in range(18)]
        Vbf = R1f[:, 0:16384].rearrange("p (l v) -> p l v", l=16)

        fence()
        for half in range(2):
            kts = [w_next() for _ in range(8)]
            kslots = [s for (_, s) in kts]
            for blk in BLOCKS_B:
                c0 = col0_of(blk)
                b0, b1 = proj_tokmajor(blk, lambda k: [("XT", blk)], lambda k: XT[:, k, c0:c0 + 128], 8, kslots)
                sl = blk % 2
                kk = ("Ksb", sl)
                act(Ksb[sl][:, 0:512], ps[:, b0, :], AF.Copy, [("ps", b0)], [kk])
                act(Ksb[sl][:, 512:1024], ps[:, b1, :], AF.Copy, [("ps", b1)], [kk])
                if half == 0:
                    rope("dve", Ksb[sl], 16, blk, 128, kk)
                    dst = kp[blk * 128:(blk + 1) * 128, :] if blk < 16 else ks
                    dma("sp", dst, Ksb[sl], [kk], [], ("ko", sl))
                    if blk < 16:
                        for hh in range(2):
                            b = nb()
                            for k in range(4):
                                h = hh * 4 + k
                                tr(ps[:, b, k * 128:(k + 1) * 128], Ksb[sl][:, h * 128:(h + 1) * 128], 128, [kk], [("ps", b)])
                            src = ps[:, b, :].rearrange("p (k n) -> p k n", k=4)
                            dstT = R1[:, hh * 4:(hh + 1) * 4, c0:c0 + 128]
                            wk = [("R1", hh * 4 + k, blk) for k in range(4)]
                            if hh == 0:
                                act(dstT, src, AF.Copy, [("ps", b)], wk)
                            else:
                                cp("dve", dstT, src, [("ps", b)], wk)
                else:
                    dst = vp[blk * 128:(blk + 1) * 128, :] if blk < 16 else vs
                    dma("sp", dst, Ksb[sl], [kk], [], ("ko", sl))
                    if blk < 16:
                        cp("dve", Vbf[:, blk, :], Ksb[sl], [kk], R1ALL + [("Vbf", blk)])
            w_done(kts[-1][0])
            if half == 0:
                for h in range(NH):
                    dma("sp", kvb[h][0:128, :], R1[:, h, 0:2048], [("R1", h, b) for b in range(16)], [("kvb", h)], ("kb", h % 4))
            else:
                for h in range(NH):
                    dma("sp", kvb[h][128:256, :].rearrange("p (l v) -> p l v", l=16), Vbf[:, :, h * 128:(h + 1) * 128],
                        R1ALL + [("Vbf", b) for b in range(16)], [("kvb", h)], ("kb", h % 4))

        qsb = Ksb
        KP = [R2b[:, s * 2048:(s + 1) * 2048].rearrange("p (r k) -> p r k", r=4) for s in range(2)]
        VP = [R2b[:, 4096 + s * 2048:4096 + (s + 1) * 2048].rearrange("p (r k) -> p r k", r=4) for s in range(2)]
        EB = [R2b[:, 8192 + s * 1024:8192 + (s + 1) * 1024].rearrange("p (c q) -> p c q", c=2) for s in range(4)]
        EACC = R2[:, 6144:7168].rearrange("p (c q) -> p c q", c=2)
        RZ = R2[:, 7168:7680]
        OO = [R2[:, 7680:8192]]

        for i in range(n_layers_b):
            l = 2 + i
            lam_init = 0.8 - 0.6 * math.exp(-0.3 * l)
            lq1 = lamt[:, (0 * 2 + i) * 64:(0 * 2 + i + 1) * 64]
            lk1 = lamt[:, (1 * 2 + i) * 64:(1 * 2 + i + 1) * 64]
            lq2 = lamt[:, (2 * 2 + i) * 64:(2 * 2 + i + 1) * 64]
            lk2 = lamt[:, (3 * 2 + i) * 64:(3 * 2 + i + 1) * 64]
            tt("dve", RT[:, 0:64], lq1, lk1, ALU.mult, ["lamt", "rt0"], ["rt0"])
            P.op("dve", lambda e: e.tensor_reduce(out=SM[:, 24:25], in_=RT[:, 0:64], axis=AX.X, op=ALU.add), reads=["rt0"], writes=["lam1"])
            tt("dve", RT[:, 64:128], lq2, lk2, ALU.mult, ["lamt", "rt0"], ["rt0b"])
            P.op("dve", lambda e: e.tensor_reduce(out=SM[:, 25:26], in_=RT[:, 64:128], axis=AX.X, op=ALU.add), reads=["rt0b"], writes=["lam2"])
            act(SM[:, 24:26], SM[:, 24:26], AF.Exp, ["lam1", "lam2"], ["lame"])
            tt("dve", SM[:, 26:27], SM[:, 24:25], SM[:, 25:26], ALU.subtract, ["lame"], ["lam"])
            ts("dve", SM[:, 36 + i:37 + i], SM[:, 26:27], float(lam_init), ALU.add, ["lam"], [("nlam", i)], s2=-1.0, op1=ALU.mult)
            nlam = SM[:, 36 + i:37 + i]
            ts("dve", SM[:, 28 + i:29 + i], gcol[:, i:i + 1], float(math.sqrt(128.0) * (1.0 - lam_init)), ALU.mult, ["gcol"], [("gsc", i)])
            gsc = SM[:, 28 + i:29 + i]

            fence()
            tqh, sqh = w_next()
            bq = nb()
            for k in range(8):
                mm(ps[:, bq, 0:128], XT[:, k, SAMP0:SAMP0 + 128], Wt[:, sqh, k * 128:(k + 1) * 128], k == 0, k == 7,
                   [("XT", 16), ("W", sqh)], [("ps", bq)])
            act(QH[:, :], ps[:, bq, 0:128], AF.Copy, [("ps", bq)], ["QH"])
            rope("dve", QH[:, :], 2, 16, 128, "QH")
            if i == 0:
                tkh, skh = w_next()
                tvh, svh = w_next()
                bk = nb()
                for k in range(8):
                    mm(ps[:, bk, 0:128], XT[:, k, SAMP0:SAMP0 + 128], Wt[:, skh, k * 128:(k + 1) * 128], k == 0, k == 7,
                       [("XT", 16), ("W", skh)], [("ps", bk)])
                act(KH[:, :], ps[:, bk, 0:128], AF.Copy, [("ps", bk)], ["KH"])
                rope("dve", KH[:, :], 2, 16, 128, "KH")
                bv = nb()
                for k in range(8):
                    mm(ps[:, bv, 0:128], XT[:, k, SAMP0:SAMP0 + 128], Wt[:, svh, k * 128:(k + 1) * 128], k == 0, k == 7,
                       [("XT", 16), ("W", svh)], [("ps", bv)])
                act(VH[:, :], ps[:, bv, 0:128], AF.Copy, [("ps", bv)], ["VH"])
                cp("dve", VHb[:, :], VH[:, :], ["VH"], ["VHb"])
                w_done(tvh)
            else:
                w_done(tqh)
            qts = [w_next() for _ in range(8)]
            qslots = [s for (_, s) in qts]
            for blk in range(16):
                c0 = col0_of(blk)
                b0, b1 = proj_tokmajor(blk, lambda k: [("XT", blk)], lambda k: XT[:, k, c0:c0 + 128], 8, qslots)
                sl = blk % 2
                kk = ("Ksb", sl)
                act(qsb[sl][:, 0:512], ps[:, b0, :], AF.Copy, [("ps", b0)], [kk])
                act(qsb[sl][:, 512:1024], ps[:, b1, :], AF.Copy, [("ps", b1)], [kk])
                rope("dve", qsb[sl], 16, blk, 128, kk)
                for hh in range(2):
                    b = nb()
                    for k in range(4):
                        h = hh * 4 + k
                        tr(ps[:, b, k * 128:(k + 1) * 128], qsb[sl][:, h * 128:(h + 1) * 128], 128, [kk], [("ps", b)])
                    src = ps[:, b, :].rearrange("p (k n) -> p k n", k=4)
                    dstT = XT[:, hh * 4:(hh + 1) * 4, c0:c0 + 128]
                    if hh == 0:
                        act(dstT, src, AF.Copy, [("ps", b)], [("XT", blk)])
                    else:
                        cp("dve", dstT, src, [("ps", b)], [("XT", blk)])
            w_done(qts[-1][0])
            QT = XT

            if i == 0:
                for h in range(NH):
                    P.op("pool", lambda e, h=h: e.collective_compute("AllGather", ALU.bypass, replica_groups=[[0, 1, 2, 3], [4, 5, 6, 7]],
                                                                      ins=[kvb[h]], outs=[kvg[h]]),
                         reads=[("kvb", h)], writes=[("kvg", h)], dma=("cc", h), inc=1)
            fence()
            XS = X[:, 0:8, :].rearrange("p a b -> p (a b)")
            XSb = XS.bitcast(BF16)
            SAKEYS = [("KN", 0), ("KN", 1), ("VN", 0), ("VN", 1), "PR", "PRsq", "PRr", ("SS", 0), ("SS", 1), ("ENs", 0), ("ENs", 1), "DM", "OSB",
                      ("QBC", 0), ("QBC", 1), "OSsb"]
            for b in range(8):
                dma("sp", xsp[b * 128:(b + 1) * 128, :], X[:, b, :], [("X", b)], [("xsp", b)], ("xsp", b % 2))
            P.op("dve", lambda e: e.memset(SM[:, 41:42], 0.0), reads=[], writes=[("X", b) for b in range(8)] + SAKEYS)
            KN = [XSb[:, s * 2048:(s + 1) * 2048] for s in range(2)]
            VN = [XSb[:, 4096 + s * 2048:4096 + (s + 1) * 2048] for s in range(2)]
            PR = XS[:, 4096:6144]
            QBC = [XS[:, 6144 + s * 128:6144 + (s + 1) * 128] for s in range(2)]
            SS = XS[:, 6400:6432]
            EN = XSb[:, 12864:12896].rearrange("p (s c) -> p s c", c=2)
            ESUM = XS[:, 6448:6450]
            DM = XSb[:, 12928:13184].rearrange("p (c n) -> p c n", c=2)
            OSsb = XS[:, 6592:7104].rearrange("p (n k) -> p n k", k=4)
            FIN = XS[:, 7104:7616]
            OSB = XSb[:, 15232:15360]
            SAB = 7
            dma("sp", qhd[i], QH[:, :], ["QH"], [("qhd", i)], "qhd")
            tt("dve", PR[:, 0:128], QH[:, :], KH[:, :], ALU.mult, ["QH", "KH"], ["PR"])
            P.op("dve", lambda e: e.tensor_reduce(out=SM[:, 32:34], in_=PR[:, 0:128].rearrange("p (c d) -> p c d", c=2), axis=AX.X, op=ALU.add),
                 reads=["PR"], writes=["snew"])
            act(SM[:, 32:34], SM[:, 32:34], AF.Exp, ["snew"], ["enew"], scale=SCALE)
            for c in range(2):
                ts("dve", DM[:, c, :], identf[:, :], SM[:, 32 + c:33 + c], ALU.mult, ["identf", "enew"], ["DM"])

            def sa_gather_k(n):
                sl = n % 2
                P.op("pool", lambda e: e.indirect_dma_start(out=KN[sl], out_offset=None, in_=ckd,
                                                             in_offset=bass.IndirectOffsetOnAxis(ap=IDX[:, n:n + 1], axis=0)),
                     reads=["IDX"], writes=[("KN", sl)], dma=("gk", sl))
                dma("sp", QBC[sl], qhd[i][n:n + 1, :].partition_broadcast(128), [("qhd", i)], [("QBC", sl)], ("qbc", sl))

            def sa_gather_v(n):
                sl = n % 2
                P.op("pool", lambda e: e.indirect_dma_start(out=VN[sl], out_offset=None, in_=cvd,
                                                             in_offset=bass.IndirectOffsetOnAxis(ap=IDX[:, n:n + 1], axis=0)),
                     reads=["IDX"], writes=[("VN", sl)], dma=("gv", sl))

            ENs = [XSb[:, 12864 + s_ * 32:12896 + s_ * 32].rearrange("p (s c) -> p s c", c=2) for s_ in range(2)]
            ESUMs = [XS[:, 7680 + s_ * 2:7682 + s_ * 2] for s_ in range(2)]
            SSs = [XS[:, 6400:6432], XS[:, 7700:7732]]

            def sa_front(n):
                sl = n % 2
                if n == 0:
                    sa_gather_k(0)
                if n + 1 < 128:
                    sa_gather_k(n + 1)
                tt("dve", PR.rearrange("p (s f) -> p s f", s=16), KN[sl].rearrange("p (s f) -> p s f", s=16),
                   QBC[sl].unsqueeze(1).broadcast_to([128, 16, 128]), ALU.mult, [("KN", sl), ("QBC", sl)], ["PR"])
                P.op("dve", lambda e: e.tensor_reduce(out=SSs[sl], in_=PR.rearrange("p (g d) -> p g d", d=64), axis=AX.X, op=ALU.add),
                     reads=["PR"], writes=[("SS", sl)])

            def sa_mid(n):
                sl = n % 2
                sa_gather_v(n)
                SSv = SSs[sl].rearrange("p (s c) -> p s c", c=2)
                for c in range(2):
                    act(ENs[sl][:, :, c], SSv[:, :, c], AF.Exp, [("SS", sl)], [("ENs", sl)], scale=SCALE, accum_out=ESUMs[sl][:, c:c + 1])

            def sa_back(n):
                sl = n % 2
                mm(ps[:, SAB, 0:2], VHb[:, :], DM[:, :, n], True, False, ["VHb", "DM"], [("ps", SAB)])
                for s16 in range(16):
                    mm(ps[:, SAB, 0:2], VN[sl][:, s16 * 128:(s16 + 1) * 128], ENs[sl][:, s16, :], False, s16 == 15,
                       [("VN", sl), ("ENs", sl)], [("ps", SAB)])
                mm(ps[:, SAB, 2:4], onesf[:, :], ESUMs[sl], True, False, [("ENs", sl), "onesf"], [("ps", SAB)])
                mm(ps[:, SAB, 2:4], onesb[:, :], DM[:, :, n], False, True, ["onesb", "DM"], [("ps", SAB)])
                act(OSsb[:, n, :], ps[:, SAB, 0:4], AF.Copy, [("ps", SAB)], ["OSsb"])

            N_SLOTS = 130

            def sa_slot(k):
                if 2 <= k < 130:
                    sa_back(k - 2)
                if 1 <= k < 129:
                    sa_mid(k - 1)
                if k < 128:
                    sa_front(k)

            bank_ctr[0] = 0
            SB = [(0, 1), (2, 3)]
            OB = (4, 5)
            ZBK = 6
            step = 0
            piece = [0]
            sa_next = [0]
            for h in range(NH):
                for qc in range(4):
                    nkb = 16 * (qc + 1)
                    steps = []
                    for lp in range(qc + 1):
                        for a in range(4):
                            lk = lp * 4 + a
                            for r in range(4):
                                steps.append((lp, a, lk, r))
                    pslots = {}

                    def emit_qk(si):
                        lp, a, lk, r = steps[si]
                        if lp not in pslots:
                            pslot = piece[0] % 2
                            piece[0] += 1
                            pslots[lp] = pslot
                            ksrc = kvg[h].rearrange("(r x) c -> x r c", x=256)
                            dma("sp", KP[pslot], ksrc[0:128, :, lp * 512:(lp + 1) * 512], [("kvg", h)], [("KP", pslot)], ("kp", pslot))
                            dma("sp", VP[pslot], ksrc[128:256, :, lp * 512:(lp + 1) * 512], [("kvg", h)], [("VP", pslot)], ("vp", pslot))
                        pslot = pslots[lp]
                        off = max(0, lk - 4 * qc)
                        n = 512 - 128 * off
                        qc0 = (4 * qc + off) * 128
                        g = (step_base[0] + si)
                        sp_ = SB[g % 2]
                        es = g % 4
                        qkeys = [("XT", b) for b in range(4 * qc + off, 4 * qc + 4)]
                        diag = lk >= 4 * qc
                        for c in range(2):
                            mm(ps[:, sp_[c], 0:n], KP[pslot][c * 64:(c + 1) * 64, r, a * 128:(a + 1) * 128],
                               QT[c * 64:(c + 1) * 64, h, qc0:qc0 + n], True, not diag,
                               qkeys + [("KP", pslot)], [("ps", sp_[c])])
                        if diag:
                            for c in range(2):
                                mm(ps[:, sp_[c], 0:128], identb[:, :], maskneg[:, r, :], False, True,
                                   ["identb", "maskneg"], [("ps", sp_[c])])
                        act(EB[es][:, :, 0:n], ps[:, sp_[0]:sp_[0] + 2, 0:n], AF.Exp, [("ps", sp_[0]), ("ps", sp_[1])],
                            [("EB", es)], scale=SCALE)
                        aeng = "dve"
                        if si == 0:
                            cp(aeng, EACC[:, 1, :], EB[es][:, 1, :], [("EB", es)], ["EACC"])
                        else:
                            tt(aeng, EACC[:, 1, 128 * off:512], EACC[:, 1, 128 * off:512], EB[es][:, 1, 0:n], ALU.add,
                               [("EB", es), "EACC"], ["EACC"])

                    def emit_pv(si):
                        lp, a, lk, r = steps[si]
                        pslot = pslots[lp]
                        off = max(0, lk - 4 * qc)
                        n = 512 - 128 * off
                        g = (step_base[0] + si)
                        es = g % 4
                        first = (si == 0)
                        last = (si == nkb - 1)
                        for c in range(2):
                            mm(ps[:, OB[c], 128 * off:512], VP[pslot][:, r, a * 128:(a + 1) * 128], EB[es][:, c, 0:n],
                               first, last, [("EB", es), ("VP", pslot)], [("ps", OB[c])])
                        mm(ps[:, ZBK, 128 * off:512], onesb[:, :], EB[es][:, 0, 0:n], first, last, [("EB", es), "onesb"], [("ps", ZBK)])

                    step_base = [step]
                    for si in range(nkb + 1):
                        if si < nkb:
                            emit_qk(si)
                        if si >= 1:
                            emit_pv(si - 1)
                        if (step + si) % 9 == 8 and sa_next[0] < N_SLOTS:
                            sa_slot(sa_next[0])
                            sa_next[0] += 1
                    step += nkb
                    qcols = slice(qc * 512, (qc + 1) * 512)
                    act(RZ, ps[:, ZBK, :], AF.Ln, [("ps", ZBK)], ["RZ"])
                    act(RZ, RZ, AF.Exp, ["RZ"], ["RZ"], scale=-1.0)
                    tt("dve", OO[0], ps[:, OB[0], :], RZ, ALU.mult, [("ps", OB[0]), "RZ"], [("OO", 0)])
                    mm(ps[:, ZBK, :], onesf[:, :], EACC[:, 1, :], True, True, ["EACC", "onesf"], [("ps", ZBK)])
                    act(RZ, ps[:, ZBK, :], AF.Ln, [("ps", ZBK)], ["RZ"])
                    act(RZ, RZ, AF.Exp, ["RZ"], ["RZ"], scale=-1.0)
                    tt("dve", RZ, ps[:, OB[1], :], RZ, ALU.mult, [("ps", OB[1]), "RZ"], ["RZ"])
                    stt("dve", OO[0], RZ, nlam, OO[0], ALU.mult, ALU.add, [("OO", 0), "RZ", ("nlam", i)], [("OO", 0)])
                    tt("dve", RZ, OO[0], OO[0], ALU.mult, [("OO", 0)], ["RZ"])
                    mm(ps[:, ZBK, :], onesf[:, :], RZ, True, True, ["RZ", "onesf"], [("ps", ZBK)])
                    act(RZ, ps[:, ZBK, :], AF.Ln, [("ps", ZBK), "eps"], ["RZ"], bias=SM[:, 17:18], scale=1.0)
                    act(RZ, RZ, AF.Exp, ["RZ"], ["RZ"], scale=-0.5)
                    tt("dve", OO[0], OO[0], RZ, ALU.mult, [("OO", 0), "RZ"], [("OO", 0)])
                    act(R1[:, h, qcols], OO[0], AF.Identity, [("OO", 0), ("gsc", i)], [("R1", h, b) for b in range(4 * qc, 4 * qc + 4)], scale=gsc)
            while sa_next[0] < N_SLOTS:
                sa_slot(sa_next[0])
                sa_next[0] += 1

            F0 = FIN[:, 0:256].rearrange("p (n k) -> p n k", k=2)
            P.op("dve", lambda e: e.reciprocal(out=F0, in_=OSsb[:, :, 2:4]), reads=["OSsb"], writes=["PR"])
            tt("dve", F0, OSsb[:, :, 0:2], F0, ALU.mult, ["OSsb", "PR"], ["PR"])
            stt("dve", FIN[:, 256:384], F0[:, :, 1], nlam, F0[:, :, 0], ALU.mult, ALU.add, ["PR", ("nlam", i)], ["PRo"])
            tt("dve", FIN[:, 384:512], FIN[:, 256:384], FIN[:, 256:384], ALU.mult, ["PRo"], ["PRsq"])
            mm(ps[:, SAB, 0:128], onesf[:, :], FIN[:, 384:512], True, True, ["PRsq", "onesf"], [("ps", SAB)])
            act(FIN[:, 384:512], ps[:, SAB, 0:128], AF.Sqrt, [("ps", SAB), "eps"], ["PRr"], bias=SM[:, 17:18], scale=1.0)
            P.op("dve", lambda e: e.reciprocal(out=FIN[:, 384:512], in_=FIN[:, 384:512]), reads=["PRr"], writes=["PRr"])
            tt("dve", FIN[:, 256:384], FIN[:, 256:384], FIN[:, 384:512], ALU.mult, ["PRo", "PRr"], ["PRo"])
            act(OSB, FIN[:, 256:384], AF.Identity, ["PRo", ("gsc", i)], ["OSB"], scale=gsc)
            dma("sp", obd[i], OSB, ["OSB"], [("ob", i)], "ob")
            P.op("pool", lambda e, i=i: e.collective_compute("AllGather", ALU.bypass, replica_groups=[[0, 1, 2, 3], [4, 5, 6, 7]],
                                                              ins=[obd[i]], outs=[og4d[i]]),
                 reads=[("ob", i)], writes=[("og4", i)], dma=("cco4", i), inc=1)
            P.op("pool", lambda e, i=i: e.collective_compute("AllGather", ALU.bypass, replica_groups=[[0, 4], [1, 5], [2, 6], [3, 7]],
                                                              ins=[og4d[i]], outs=[ogd[i]]),
                 reads=[("og4", i)], writes=[("og", i)], dma=("cco", i), inc=1)
            dma("sp", R1[:, :, SAMP0:SAMP0 + 128], ogd[i].rearrange("(h v) n -> v h n", v=128), [("og", i)],
                [("R1", h, 16) for h in range(8)], "ogl")
            P.op("dve", lambda e: e.memset(SM[:, 42:43], 0.0), reads=[], writes=SAKEYS + ["PRo", "xrfence"])
            for b in range(8):
                dma("sp", X[:, b, :], xsp[b * 128:(b + 1) * 128, :], [("xsp", b), "xrfence"], [("X", b)], ("xsp", b % 2))

            fence()
            load_gb(ln1_g, ln1_b, l)
            ots = [w_next() for _ in range(8)]
            oslots = [s for (_, s) in ots]
            for blk in BLOCKS_B:
                c0 = col0_of(blk)
                b0, b1 = proj_tokmajor(blk, lambda k: [("R1", k, blk)], lambda k: R1[:, k, c0:c0 + 128], 8, oslots)
                resid_accum(blk, b0, b1, first=True)
                layer_norm(blk)
                make_XT(blk)
            w_done(ots[-1][0])
            ffn(l, BLOCKS_B, TILES_B, last_layer=(i == n_layers_b - 1))

        for blk in range(16):
            dma("sp", yp[blk * 128:(blk + 1) * 128, :], X[:, blk, :], [("X", blk)], [], ("yo", blk % 4))
        dma("sp", ys, X[:, 16, :], [("X", 16)], [], ("yo", 0))
        P.emit(nc, st)
    return nc, P


_CACHE = {}


def _get_program():
    if "nc" not in _CACHE:
        _CACHE["nc"], _CACHE["P"] = build_program()
    return _CACHE["nc"]


def kernel(x_prompt, x_sample, cache_k, cache_v, state_conv, page_table,
           w_in, conv_w, w_mix_out, w_kv, w_q, w_o,
           lambda_q1, lambda_k1, lambda_q2, lambda_k2, subln_g,
           ln1_g, ln1_b, w_gate, w_up, w_down, ln2_g, ln2_b):
    f32 = np.float32
    A = lambda a: np.ascontiguousarray(np.asarray(a), dtype=f32)
    x_prompt = A(x_prompt); x_sample = A(x_sample); state_conv = A(state_conv)
    cache_k = np.asarray(cache_k); cache_v = np.asarray(cache_v)
    page_table = np.asarray(page_table).astype(np.int32)
    w_in = A(w_in); conv_w = A(conv_w); w_mix_out = A(w_mix_out); w_kv = A(w_kv); w_q = A(w_q); w_o = A(w_o)
    w_gate = A(w_gate); w_up = A(w_up); w_down = A(w_down)
    ln1_g = A(ln1_g); ln1_b = A(ln1_b); ln2_g = A(ln2_g); ln2_b = A(ln2_b); subln_g = A(subln_g)
    lam = np.concatenate([A(lambda_q1).reshape(-1), A(lambda_k1).reshape(-1),
                          A(lambda_q2).reshape(-1), A(lambda_k2).reshape(-1)]).reshape(1, 512)
    cw = np.ascontiguousarray(conv_w.reshape(2, 3, 8, 128).transpose(3, 0, 1, 2).reshape(128, 48))
    ptrep = np.ascontiguousarray(np.repeat(page_table.T, 8, axis=0))
    pmod = (np.arange(128) % 8).astype(f32).reshape(128, 1)
    xs = x_sample.reshape(128, D)
    p_idx = np.arange(128)
    in_maps = []
    for r in range(8):
        s, j = r // 4, r % 4
        xp = np.empty((2048, D), f32)
        xh = np.zeros((64, D), f32)
        hv = np.zeros((128, 64), f32)
        pos = np.zeros((128, 18), f32)
        for l in range(NB):
            t0 = (4 * l + j) * 128
            xp[l * 128:(l + 1) * 128] = x_prompt[s, t0:t0 + 128]
            if t0 >= 4:
                xh[l * 4:(l + 1) * 4] = x_prompt[s, t0 - 4:t0]
                hv[:, l * 4:(l + 1) * 4] = 1.0
            pos[:, l] = t0 + p_idx
        pos[:, 16] = 2048.0
        masks = np.zeros((128, 4, 128), f32)
        for rr in range(4):
            masks[:, rr, :] = ((rr * 128 + p_idx)[:, None] <= (j * 128 + p_idx)[None, :]).astype(f32)
        wq_h = np.ascontiguousarray(w_q[:, :, r * 128:(r + 1) * 128])
        wkv_h = np.ascontiguousarray(np.concatenate([w_kv[:, r * 128:(r + 1) * 128], w_kv[:, D + r * 128:D + (r + 1) * 128]], axis=1))
        ck = np.ascontiguousarray(cache_k[:, :, r], dtype=f32).reshape(N_POOL * 8, 2048)
        cv = np.ascontiguousarray(cache_v[:, :, r], dtype=f32).reshape(N_POOL * 8, 2048)
        in_maps.append(dict(
            xp=xp, xs=xs, xh=xh, hv=hv, pos=pos, masks=masks.reshape(128, 512), pmod=pmod,
            w_in=w_in, cw=cw, w_mix_out=w_mix_out, w_kv=w_kv, w_q=w_q, w_o=w_o,
            w_gate=w_gate, w_up=w_up, w_down=w_down,
            ln1_g=ln1_g, ln1_b=ln1_b, ln2_g=ln2_g, ln2_b=ln2_b, subln_g=subln_g, lam=lam,
            wq_h=wq_h, wkv_h=wkv_h, ck=ck, cv=cv, ptrep=ptrep, state_conv=state_conv))
    nc = _get_program()
    res = run_bass_kernel_spmd(nc, in_maps, core_ids=list(range(8))).results
    y_prompt = np.empty((2, 8192, D), f32)
    k_prompt = np.empty((2, 8192, D), f32)
    v_prompt = np.empty((2, 8192, D), f32)
    conv_prompt = np.empty((2, 2, 2, D), f32)
    for r in range(8):
        s, j = r // 4, r % 4
        for l in range(NB):
            t0 = (4 * l + j) * 128
            y_prompt[s, t0:t0 + 128] = res[r]["yp"][l * 128:(l + 1) * 128]
            k_prompt[s, t0:t0 + 128] = res[r]["kp"][l * 128:(l + 1) * 128]
            v_prompt[s, t0:t0 + 128] = res[r]["vp"][l * 128:(l + 1) * 128]
        if j == 3:
            conv_prompt[:, s] = res[r]["cpo"]
    y_sample = np.asarray(res[0]["ys"], f32).reshape(128, 1, D)
    k_sample = np.asarray(res[0]["ks"], f32).reshape(128, 1, 8, 2, 64)
    v_sample = np.asarray(res[0]["vs"], f32).reshape(128, 1, 8, 128)
    conv_sample = np.asarray(res[0]["cso"], f32).reshape(2, 128, 2, D)
    return (y_prompt, y_sample, k_prompt.reshape(2, 8192, 8, 2, 64), v_prompt.reshape(2, 8192, 8, 128),
            conv_prompt, k_sample, v_sample, conv_sample)
```
